# Optimizing a Trainium2 kernel written in Bass

```python
import math
import jax, jax.numpy as jnp
from jax import lax
import numpy as np

D_MODEL = 1024
BATCH = 32
SEQ = 2048
DEPTH = 1

CTX_LEN = 256
GRID_W = 64
D_MIX = D_MODEL
RW_N = 64
RW_W = D_MIX // 2
RW_H = RW_W // RW_N
LORA_W = 64
LORA_A = 64
LORA_G = 128
TSHIFT = 3
M_INNER = D_MIX - RW_W
M_P = 64
M_H = M_INNER // M_P
M_G = 2
M_N = 128
CHUNK = 128
M_CONV = 3
M_CONV_CH = M_INNER + 2 * M_G * M_N
RW_SIZES = (RW_W, RW_W, RW_W, LORA_W, LORA_W, LORA_A, LORA_A, LORA_G)
M_SIZES = (M_INNER, M_CONV_CH, M_H, M_H)
RW_COLS = sum(RW_SIZES)
M_COLS = sum(M_SIZES)
IN_COLS = RW_COLS + M_COLS
D_FF = ((8 * D_MODEL + 3 * 256 - 1) // (3 * 256)) * 256
N_MOD = 6
NORM_EPS = 1e-6
RW_LN_EPS = 64e-5

kernel_name = 'rwkv7_mamba2_hybrid_dit_block'


def rms_normalize(x, eps=NORM_EPS):
    xf = x.astype(jnp.float32)
    return (xf * lax.rsqrt(jnp.mean(xf * xf, axis=-1, keepdims=True) + eps)).astype(x.dtype)


def rmsnorm(x, w):
    return rms_normalize(x) * w


def modulate(h, shift, scale):
    return h * (1.0 + scale) + shift


def split_cols(z, sizes):
    return jnp.split(z, np.cumsum(sizes)[:-1].tolist(), axis=-1)


def dwconv1d(x, w):
    k = w.shape[0]
    return lax.conv_general_dilated(x, w[:, None, :], window_strides=(1,), padding=[(k // 2, k // 2)],
                                    dimension_numbers=('NWC', 'WIO', 'NWC'), feature_group_count=x.shape[-1])


def dwconv2d(x, w, rows):
    b, t, ch = x.shape
    k = w.shape[0]
    y = lax.conv_general_dilated(x.reshape(b, rows, t // rows, ch), w[:, :, None, :], window_strides=(1, 1),
                                 padding=[(k // 2, k // 2), (k // 2, k // 2)],
                                 dimension_numbers=('NHWC', 'HWIO', 'NHWC'), feature_group_count=ch)
    return y.reshape(b, t, ch)


def swiglu(h, w_gu, w_down):
    gate, up = jnp.split(h @ w_gu, 2, axis=-1)
    return (jax.nn.silu(gate) * up) @ w_down


def rwkv_scan(r, w, k, v, kk, a, s0, reverse, emit):
    kka = kk * a

    def step(s, inp):
        w_t, k_t, v_t, kk_t, kka_t, *rest = inp
        sa = jnp.einsum('bhij,bhj->bhi', s, kk_t)
        s = s * w_t[:, :, None, :] - sa[..., None] * kka_t[:, :, None, :] + v_t[..., None] * k_t[:, :, None, :]
        y = jnp.einsum('bhij,bhj->bhi', s, rest[0]) if emit else None
        return s, y

    seqs = (w, k, v, kk, kka) + ((r,) if emit else ())
    s, y = lax.scan(step, s0, tuple(jnp.swapaxes(z, 0, 1) for z in seqs), reverse=reverse)
    return (jnp.swapaxes(y, 0, 1) if emit else None), s


def rwkv_branch(u, s0_f, s0_b, emit, tshift_w, w0, w2, a0, a2, g2, k_k, k_a, r_k, lnx_w, lnx_b):
    b, t, _ = u.shape
    u = dwconv1d(u, tshift_w)
    r, k, v, wd_f, wd_b, ad_f, ad_b, gd = split_cols(u, RW_SIZES)
    heads = lambda z: z.reshape(b, t, RW_H, RW_N)
    kkf = heads(k * k_k).astype(jnp.float32)
    kk = (kkf / jnp.maximum(jnp.sqrt(jnp.sum(kkf * kkf, -1, keepdims=True)), 1e-12)).astype(u.dtype)
    ys, finals = [], []
    for d, (wd, ad, s0) in enumerate(((wd_f, ad_f, s0_f), (wd_b, ad_b, s0_b))):
        log_w = -jax.nn.softplus(-(w0[d] + jnp.tanh(wd) @ w2[d])) - 0.5
        a = jax.nn.sigmoid(a0[d] + ad @ a2[d])
        k_d = k * (1.0 + (a - 1.0) * k_a)
        y, s = rwkv_scan(heads(r), heads(jnp.exp(-jnp.exp(log_w))), heads(k_d), heads(v), kk, heads(a),
                         s0, d == 1, emit)
        ys.append(y)
        finals.append(s)
    if not emit:
        return None, finals[0], finals[1]
    yf = (ys[0] + ys[1]).astype(jnp.float32)
    mu = jnp.mean(yf, -1, keepdims=True)
    var = jnp.mean(jnp.square(yf - mu), -1, keepdims=True)
    y = ((yf - mu) * lax.rsqrt(var + RW_LN_EPS)).astype(u.dtype).reshape(b, t, RW_W) * lnx_w + lnx_b
    bonus = jnp.sum(heads(r) * heads(k) * r_k, -1, keepdims=True) * heads(v)
    g = jax.nn.sigmoid(gd) @ g2
    return (y + bonus.reshape(b, t, RW_W)) * g, finals[0], finals[1]


def segsum_from_cumsum(cs):
    n = cs.shape[-1]
    diff = cs[..., :, None] - cs[..., None, :]
    return jnp.where(jnp.tril(jnp.ones((n, n), dtype=bool)), diff, -jnp.inf)


def ssd(xh, log_a, bmat, cmat, h0, with_output):
    b, t, nh, p = xh.shape
    g, n = bmat.shape[2], bmat.shape[3]
    hg = nh // g
    nc = t // CHUNK
    dtp = xh.dtype
    X = xh.reshape(b, nc, CHUNK, g, hg, p)
    Bc = bmat.reshape(b, nc, CHUNK, g, n)
    A = log_a.astype(jnp.float32).reshape(b, nc, CHUNK, g, hg).transpose(0, 3, 4, 1, 2)
    a_cs = jnp.cumsum(A, axis=-1)
    decay_states = jnp.exp(a_cs[..., -1:] - a_cs).astype(dtp)
    states = jnp.einsum('bclgn,bgjcl,bclgjp->bcgjpn', Bc, decay_states, X)
    states = jnp.concatenate([h0.reshape(b, 1, g, hg, p, n), states], axis=1)
    chunk_cs = jnp.cumsum(jnp.pad(a_cs[..., -1], ((0, 0), (0, 0), (0, 0), (1, 0))), axis=-1)
    decay_chunk = jnp.exp(segsum_from_cumsum(chunk_cs)).astype(dtp)
    new_states = jnp.einsum('bgjzc,bcgjpn->bzgjpn', decay_chunk, states)
    final = new_states[:, -1].reshape(b, nh, p, n)
    if not with_output:
        return None, final
    Cc = cmat.reshape(b, nc, CHUNK, g, n)
    lmat = jnp.exp(segsum_from_cumsum(a_cs)).astype(dtp)
    cb = jnp.einsum('bclgn,bcsgn->bgcls', Cc, Bc)
    y_diag = jnp.einsum('bgjcls,bcsgjp->bclgjp', cb[:, :, None] * lmat, X)
    y_off = jnp.einsum('bclgn,bcgjpn,bgjcl->bclgjp', Cc, new_states[:, :-1], jnp.exp(a_cs).astype(dtp))
    return (y_diag + y_off).reshape(b, t, nh, p), final


def mamba_branch(u, rows, h0_f, h0_b, emit, conv_w, conv_b, dt_bias, a_log, d_skip, gnorm_w):
    b, t, _ = u.shape
    z, xbc, dt_f, dt_b = split_cols(u, M_SIZES)
    xbc = dwconv1d(xbc, conv_w[M_CONV // 2]) if rows is None else dwconv2d(xbc, conv_w, rows)
    xs, bm, cm = split_cols(jax.nn.silu(xbc + conv_b), (M_INNER, M_G * M_N, M_G * M_N))
    xh = xs.reshape(b, t, M_H, M_P)
    bm = bm.reshape(b, t, M_G, M_N)
    cm = cm.reshape(b, t, M_G, M_N)
    ys, finals = [], []
    for d, (dt_raw, h0) in enumerate(((dt_f, h0_f), (dt_b, h0_b))):
        dt = jax.nn.softplus(dt_raw + dt_bias[d])
        seqs = (xh * dt[..., None], -dt * jnp.exp(a_log[d]), bm, cm)
        if d == 1:
            seqs = tuple(s[:, ::-1] for s in seqs)
        y, hf = ssd(*seqs, h0, emit)
        ys.append(y[:, ::-1] if (emit and d == 1) else y)
        finals.append(hf)
    if not emit:
        return None, finals[0], finals[1]
    y = (ys[0] + ys[1] + d_skip[:, None] * xh).reshape(b, t, M_INNER) * jax.nn.silu(z)
    y = rms_normalize(y.reshape(b, t, M_G, M_INNER // M_G)).reshape(b, t, M_INNER) * gnorm_w
    return y, finals[0], finals[1]


def setup_inputs(seed: int = 0) -> dict:
    key = jax.random.key(seed)
    ks = iter(jax.random.split(key, 40))

    def nrm(shape, scale):
        return jax.random.normal(next(ks), shape, jnp.float32) * scale

    def unif(shape, lo, hi):
        return jax.random.uniform(next(ks), shape, jnp.float32, lo, hi)

    L = DEPTH
    x = nrm((BATCH, SEQ, D_MODEL), 1.0)
    c = nrm((BATCH, D_MODEL), 1.0)
    ctx = nrm((BATCH, CTX_LEN, D_MODEL), 1.0)
    c_ctx = nrm((D_MODEL,), 1.0)
    mod_w = nrm((L, D_MODEL, N_MOD * D_MODEL), 0.5 * D_MODEL ** -0.5)
    mod_b = nrm((L, N_MOD * D_MODEL), 0.01)
    norm1_w = 1.0 + nrm((L, D_MODEL), 0.02)
    w_in = nrm((L, D_MODEL, IN_COLS), D_MODEL ** -0.5)
    mix = unif((L, 1, RW_COLS), 0.2, 0.8)
    tshift_w = jnp.concatenate([0.5 * mix, 1.0 - mix, 0.5 * mix], axis=1) + nrm((L, TSHIFT, RW_COLS), 0.02)
    w0 = unif((L, 2, RW_W), -5.0, 0.5)
    w2 = nrm((L, 2, LORA_W, RW_W), 0.1 * LORA_W ** -0.5)
    a0 = nrm((L, 2, RW_W), 0.1)
    a2 = nrm((L, 2, LORA_A, RW_W), 0.1 * LORA_A ** -0.5)
    g2 = nrm((L, LORA_G, RW_W), LORA_G ** -0.5)
    k_k = 0.85 + nrm((L, RW_W), 0.05)
    k_a = 1.0 + nrm((L, RW_W), 0.05)
    r_k = nrm((L, RW_H, RW_N), 0.1)
    lnx_w = 1.0 + nrm((L, RW_W), 0.02)
    lnx_b = nrm((L, RW_W), 0.01)
    conv_w = nrm((L, M_CONV, M_CONV, M_CONV_CH), 1.0 / M_CONV)
    conv_b = nrm((L, M_CONV_CH), 0.01)
    dt0 = jnp.exp(unif((L, 2, M_H), math.log(1e-3), math.log(1e-1)))
    dt_bias = dt0 + jnp.log(-jnp.expm1(-dt0))
    a_log = jnp.log(unif((L, 2, M_H), 1.0, 16.0))
    d_skip = 1.0 + nrm((L, M_H), 0.1)
    gnorm_w = 1.0 + nrm((L, M_INNER), 0.02)
    w_out = nrm((L, D_MIX, D_MODEL), D_MIX ** -0.5)
    norm2_w = 1.0 + nrm((L, D_MODEL), 0.02)
    w_gu = nrm((L, D_MODEL, 2 * D_FF), D_MODEL ** -0.5)
    w_down = nrm((L, D_FF, D_MODEL), D_FF ** -0.5)
    final_norm_w = 1.0 + nrm((D_MODEL,), 0.02)
    return {'x': x, 'c': c, 'ctx': ctx, 'c_ctx': c_ctx, 'mod_w': mod_w, 'mod_b': mod_b, 'norm1_w': norm1_w,
            'w_in': w_in, 'tshift_w': tshift_w, 'w0': w0, 'w2': w2, 'a0': a0, 'a2': a2, 'g2': g2,
            'k_k': k_k, 'k_a': k_a, 'r_k': r_k, 'lnx_w': lnx_w, 'lnx_b': lnx_b, 'conv_w': conv_w,
            'conv_b': conv_b, 'dt_bias': dt_bias, 'a_log': a_log, 'd_skip': d_skip, 'gnorm_w': gnorm_w,
            'w_out': w_out, 'norm2_w': norm2_w, 'w_gu': w_gu, 'w_down': w_down, 'final_norm_w': final_norm_w}


def reference(x, c, ctx, c_ctx, mod_w, mod_b, norm1_w, w_in, tshift_w, w0, w2, a0, a2, g2, k_k, k_a, r_k,
              lnx_w, lnx_b, conv_w, conv_b, dt_bias, a_log, d_skip, gnorm_w, w_out, norm2_w, w_gu, w_down,
              final_norm_w):
    b, t, _ = x.shape
    rows = t // GRID_W
    for i in range(DEPTH):
        emit_ctx = i + 1 < DEPTH
        lat_mod = jnp.split(jax.nn.silu(c) @ mod_w[i] + mod_b[i], N_MOD, axis=-1)
        sh1, sc1, gt1, sh2, sc2, gt2 = (m[:, None, :] for m in lat_mod)
        csh1, csc1, cgt1, csh2, csc2, cgt2 = jnp.split(jax.nn.silu(c_ctx) @ mod_w[i] + mod_b[i], N_MOD, axis=-1)
        u = modulate(rmsnorm(x, norm1_w[i]), sh1, sc1) @ w_in[i]
        uc = modulate(rmsnorm(ctx, norm1_w[i]), csh1, csc1) @ w_in[i]
        rw_p = (tshift_w[i], w0[i], w2[i], a0[i], a2[i], g2[i], k_k[i], k_a[i], r_k[i], lnx_w[i], lnx_b[i])
        m_p = (conv_w[i], conv_b[i], dt_bias[i], a_log[i], d_skip[i], gnorm_w[i])
        s0 = jnp.zeros((b, RW_H, RW_N, RW_N), x.dtype)
        h0 = jnp.zeros((b, M_H, M_P, M_N), x.dtype)
        yc_rw, rs_f, rs_b = rwkv_branch(uc[..., :RW_COLS], s0, s0, emit_ctx, *rw_p)
        yc_m, ms_f, ms_b = mamba_branch(uc[..., RW_COLS:], None, h0, h0, emit_ctx, *m_p)
        y_rw, _, _ = rwkv_branch(u[..., :RW_COLS], rs_f, rs_b, True, *rw_p)
        y_m, _, _ = mamba_branch(u[..., RW_COLS:], rows, ms_f, ms_b, True, *m_p)
        x = x + gt1 * (jnp.concatenate([y_rw, y_m], axis=-1) @ w_out[i])
        x = x + gt2 * swiglu(modulate(rmsnorm(x, norm2_w[i]), sh2, sc2), w_gu[i], w_down[i])
        if emit_ctx:
            ctx = ctx + cgt1 * (jnp.concatenate([yc_rw, yc_m], axis=-1) @ w_out[i])
            ctx = ctx + cgt2 * swiglu(modulate(rmsnorm(ctx, norm2_w[i]), csh2, csc2), w_gu[i], w_down[i])
    return rmsnorm(x, final_norm_w)
```

```python
import numpy as np
import ml_dtypes
from contextlib import ExitStack
import concourse.bass as bass
import concourse.mybir as mybir
from concourse.bass_utils import run_bass_kernel_spmd

F32 = mybir.dt.float32
BF16 = mybir.dt.bfloat16
AF = mybir.ActivationFunctionType
ALU = mybir.AluOpType
AX = mybir.AxisListType

NCORES = 8
NSEQ = 4
D = 1024
SEQ = 2048
CTX = 256
T = CTX + SEQ
NCH = T // 128
IN_COLS = 3472
D_FF = 2816
EPS = 1e-6
RW_LN_EPS = 64e-5

ENGS = ("pe", "dve", "act", "pool", "sp")
NO_SELF_SYNC = ("pe",)
SEM_LIMIT = 24000
NDMASEM = 24


_UID = [0]


def SBT(nc, name, shape, dt):
    _UID[0] += 1
    return nc.sbuf_tensor("%s_%d" % (name, _UID[0]), shape, dt)


def PST(nc, name, shape, dt):
    _UID[0] += 1
    return nc.psum_tensor("%s_%d" % (name, _UID[0]), shape, dt)


class Trk:
    __slots__ = ("w", "r")

    def __init__(self):
        self.w = {}
        self.r = {}


class View:
    __slots__ = ("ap", "trks")

    def __init__(self, ap, trks):
        self.ap = ap
        self.trks = trks


class Buf:
    def __init__(self, h, n=1):
        self.h = h
        self.t = [Trk() for _ in range(n)]

    def __call__(self, idx=None, k=None):
        ap = self.h[:] if idx is None else self.h[idx]
        if k is None:
            trks = self.t
        elif isinstance(k, int):
            trks = [self.t[k]]
        else:
            trks = [self.t[i] for i in k]
        return View(ap, trks)


def V(view, f):
    return View(f(view.ap), view.trks)


class Prog:
    def __init__(self, nc):
        self.nc = nc
        self.q = {e: [] for e in ENGS}
        self.cnt = {}
        self.seen = {e: {} for e in ENGS}
        self.dma_rr = {e: 0 for e in ENGS}
        self.nops = 0

    def _emit(self, eng, fn, reads, writes, dma=False):
        deps = {}

        def need(kv):
            if kv is None:
                return
            k, v = kv
            if deps.get(k, 0) < v:
                deps[k] = v

        for t in reads:
            for kv in t.w.items():
                need(kv)
        for t in writes:
            for kv in t.w.items():
                need(kv)
            for kv in t.r.items():
                need(kv)
        if dma:
            slot = self.dma_rr[eng]
            self.dma_rr[eng] = (slot + 1) % NDMASEM
            base = ("dma", eng, slot)
            ep = self.cnt.get(base, 0) // 2000
            key = ("dma", eng, slot, ep)
            prev = self.cnt.get(base, 0)
            if prev > 0:
                pk = ("dma", eng, slot, (prev - 1) // 2000)
                need((pk, prev - ((prev - 1) // 2000) * 2000))
            self.cnt[base] = prev + 1
            val = prev + 1 - ep * 2000
        else:
            base = ("eng", eng)
            prev = self.cnt.get(base, 0)
            ep = prev // SEM_LIMIT
            key = ("eng", eng, ep)
            self.cnt[base] = prev + 1
            val = prev + 1 - ep * SEM_LIMIT
        seen = self.seen[eng]
        for k, v in deps.items():
            if k[0] == "eng" and k[1] == eng and eng in NO_SELF_SYNC:
                continue
            if seen.get(k, 0) >= v:
                continue
            seen[k] = v
            self.q[eng].append(("wait", k, v))
        self.q[eng].append(("op", fn, key))
        kv = (key, val)
        for t in reads:
            if t.r.get(key, 0) < val:
                t.r[key] = val
        for t in writes:
            if dma or any(k2[0] == "dma" for k2 in t.w):
                t.w[key] = max(t.w.get(key, 0), val)
            else:
                t.w = {key: val}
            t.r = {}
        self.nops += 1

    def op(self, eng, method, **kw):
        reads, writes = [], []
        apkw = {}
        for name, a in kw.items():
            if isinstance(a, View):
                apkw[name] = a.ap
                if name in ("out", "ap", "accum_out"):
                    writes.extend(a.trks)
                else:
                    reads.extend(a.trks)
            else:
                apkw[name] = a
        acc = kw.get("_acc")
        if method == "matmul" and kw.get("start") is False:
            pass
        fn = lambda e, m=method, k=apkw: getattr(e, m)(**k)
        self._emit(eng, fn, reads, writes)

    def dma(self, eng, out, in_, **kw):
        fn = lambda e, o=out.ap, i=in_.ap, k=kw: e.dma_start(out=o, in_=i, **k)
        self._emit(eng, fn, list(in_.trks), list(out.trks), dma=True)

    def barrier(self):
        allk = []
        for base, c in self.cnt.items():
            if c == 0:
                continue
            if base[0] == "eng":
                ep = (c - 1) // SEM_LIMIT
                allk.append((("eng", base[1], ep), c - ep * SEM_LIMIT))
            else:
                ep = (c - 1) // 2000
                allk.append((("dma", base[1], base[2], ep), c - ep * 2000))
        for eng in ENGS:
            seen = self.seen[eng]
            for k, v in allk:
                if seen.get(k, 0) >= v:
                    continue
                seen[k] = v
                self.q[eng].append(("wait", k, v))

    def finish(self):
        nc = self.nc
        self.barrier()
        sems = {}

        def sem(k):
            if k not in sems:
                sems[k] = nc.alloc_semaphore("s_" + "_".join(str(x) for x in k))
            return sems[k]

        for eng in ENGS:
            for it in self.q[eng]:
                sem(it[2] if it[0] == "op" else it[1])

        def run(e, eng):
            for it in self.q[eng]:
                if it[0] == "wait":
                    k, v = it[1], it[2]
                    e.wait_ge(sem(k), v * 16 if k[0] == "dma" else v)
                else:
                    fn, key = it[1], it[2]
                    fn(e).then_inc(sem(key), 16 if key[0] == "dma" else 1)

        with nc.Block() as block:
            @block.tensor
            def _(e):
                run(e, "pe")

            @block.vector
            def _(e):
                run(e, "dve")

            @block.scalar
            def _(e):
                run(e, "act")

            @block.gpsimd
            def _(e):
                run(e, "pool")

            @block.sync
            def _(e):
                run(e, "sp")
        return len(sems)


class Ctx:
    pass


def build(nseq=NSEQ, debug=None, stop=None):
    nc = bass.Bass("TRN2", target_bir_lowering=False)
    P = Prog(nc)
    es = ExitStack()
    G = Ctx()
    G.nc, G.P, G.nseq, G.debug = nc, P, nseq, debug

    def din(name, shape, dt=F32):
        return Buf(nc.dram_tensor(name, list(shape), dt, kind="ExternalInput"))

    def dscr(name, shape, dt=F32, kind="Internal"):
        return Buf(nc.dram_tensor(name, list(shape), dt, kind=kind))

    G.din, G.dscr = din, dscr
    I = {}
    I["xall"] = din("xall", [nseq, T, D])
    I["cT"] = din("cT", [128, 8, 5])
    I["mod_w"] = din("mod_w", [D, 6 * D])
    I["mod_bT"] = din("mod_bT", [128, 48])
    I["mod_b"] = din("mod_b", [1, 6 * D])
    I["norm1T"] = din("norm1T", [128, 8])
    I["norm2T"] = din("norm2T", [128, 8])
    I["w_in"] = din("w_in", [D, IN_COLS])
    I["ident"] = din("ident", [128, 128])
    for nm, shp in (("tshiftT", [128, 15, 3]), ("w0T", [128, 4, 2]), ("a0T", [128, 4, 2]), ("kkT", [128, 4]),
                    ("kaT", [128, 4]), ("rkT", [128, 4]), ("lnwT", [128, 4]), ("lnbT", [128, 4]),
                    ("w2s", [128, 512]), ("a2s", [128, 512]), ("g2", [128, 512]), ("bones", [128, 128]),
                    ("masks", [128, 4, 128]), ("rmask", [128, 512]), ("convT", [128, 8, 9]), ("convbT", [128, 8]),
                    ("dtb", [128, 16]), ("alog", [128, 16]), ("dskipT", [128, 4]), ("gnwT", [128, 4]),
                    ("w_out", [D, D]), ("w_gu", [D, 2 * D_FF]), ("w_down", [D_FF, D]), ("fnw", [1, D])):
        I[nm] = din(nm, shp)
    G.I = I

    def sb(name, shape, dt=F32, n=1):
        return Buf(es.enter_context(SBT(nc, name, list(shape), dt)), n)

    def ps(name, shape, dt=F32, n=1):
        return Buf(es.enter_context(PST(nc, name, list(shape), dt)), n)

    G.sb, G.ps = sb, ps

    G.ident_f = sb("ident_f", [128, 128])
    G.ident_b = sb("ident_b", [128, 128], BF16)
    G.modT = sb("modT", [128, 48, 5])
    G.scale1 = sb("scale1", [128, 8, 5])
    G.scale2 = sb("scale2", [128, 8, 5])
    G.gts = dscr("gts", [nseq, 2, 128, D], F32)
    G.epsc = sb("epsc", [128, 1])
    P.op("dve", "memset", ap=G.epsc(), constant=EPS)
    P.dma("sp", G.ident_f(), I["ident"]())
    P.op("dve", "tensor_copy", out=G.ident_b(), in_=G.ident_f())

    G.uT = dscr("uT", [nseq, 27 * 128, T], BF16)
    G.uTf = dscr("uTf", [nseq, 3 * 128, T], F32)
    G.dtt = dscr("dtt", [nseq, T, 16], F32)

    es_A = ExitStack()
    G.w_in_bf = Buf(es_A.enter_context(SBT(nc, "w_in_bf", [128, 8, IN_COLS], BF16)))
    wv = V(I["w_in"](), lambda a: a.rearrange("(k p) n -> p k n", p=128))
    for k in range(8):
        for c0 in range(0, IN_COLS, 1736):
            P.dma("pool", G.w_in_bf((slice(None), k, slice(c0, c0 + 1736))), V(wv, lambda a: a[:, k, c0:c0 + 1736]))
    phase0(G)
    phaseA(G)
    es_A.close()

    if debug == "A":
        o1 = dscr("o_uT", [27 * 128, T], BF16, kind="ExternalOutput")
        o2 = dscr("o_uTf", [3 * 128, T], F32, kind="ExternalOutput")
        o3 = dscr("o_dtt", [T, 16], F32, kind="ExternalOutput")
        o4 = dscr("o_modT", [128, 48 * 5], F32, kind="ExternalOutput")
        o5 = dscr("o_gtbc", [128, 2 * D], F32, kind="ExternalOutput")
        P.barrier()
        P.dma("sp", o1(), G.uT((0,)))
        P.dma("sp", o2(), G.uTf((0,)))
        P.dma("sp", o3(), G.dtt((0,)))
        P.dma("sp", o4(), V(G.modT(), lambda a: a.rearrange("p a b -> p (a b)")))
        P.dma("sp", V(o5(), lambda a: a.rearrange("p (a b) -> a p b", a=2)), G.gts((0,)))

    G.x1s = dscr("x1s", [nseq, SEQ, D], F32)
    if debug is None:
        G.out = dscr("out", [nseq, SEQ, D], F32, kind="ExternalOutput")
        es_mix = ExitStack()
        rw_setup(G, es_mix)
        mb_setup(G, es_mix)
        P.barrier()
        for b in range(nseq):
            if stop == "A":
                break
            seq_stack_b1(G)
            phaseB1a(G, b)
            if stop != "B1a":
                phaseB1b(G, b)
                phaseB1c(G, b)
            G.es_seq.close()
            if stop in ("B1", "B1a"):
                continue
            seq_stack_b2(G)
            phaseB2a(G, b)
            if stop != "B2a":
                phaseB2b(G, b)
                phaseB2c(G, b)
            G.es_seq.close()
            if stop in ("B2", "B2a"):
                continue
            phaseB3(G, b)
        es_mix.close()
        if stop is None:
            phaseC(G)
    nsem = P.finish()
    es.close()
    G.nsem = nsem
    return nc, G


def phase0(G):
    P, I, sb, ps, nseq = G.P, G.I, G.sb, G.ps, G.nseq
    with ExitStack() as es:
        def sbl(name, shape, dt=F32, n=1):
            return Buf(es.enter_context(SBT(G.nc, name, list(shape), dt)), n)

        def psl(name, shape, dt=F32, n=1):
            return Buf(es.enter_context(PST(G.nc, name, list(shape), dt)), n)

        cT = sbl("cT_sb", [128, 8, 5])
        sc = sbl("sc_sb", [128, 8, 5])
        screp = sbl("screp", [128, nseq, 8, 128])
        mbT = sbl("mbT", [128, 48])
        mbrow = sbl("mbrow", [128, 2, D])
        n1 = sbl("n1", [128, 8])
        n2 = sbl("n2", [128, 8])
        mw = [sbl("mw%d" % i, [128, 8, D]) for i in range(2)]
        pm = [psl("pm%d" % i, [128, 8, 5]) for i in range(2)]
        pg = [psl("pg%d" % i, [128, 512]) for i in range(2)]
        gst = [sbl("gst%d" % i, [128, 512]) for i in range(2)]
        P.dma("sp", cT(), I["cT"]())
        P.dma("sp", mbT(), I["mod_bT"]())
        P.dma("sp", n1(), I["norm1T"]())
        P.dma("sp", n2(), I["norm2T"]())
        for qi, q in enumerate((2, 5)):
            P.dma("sp", mbrow((slice(None), qi)),
                  V(I["mod_b"]((slice(None), slice(q * D, (q + 1) * D))), lambda a: a.partition_broadcast(128)))
        P.op("act", "activation", out=sc(), in_=cT(), func=AF.Silu)
        for b in range(nseq):
            P.op("dve", "tensor_copy", out=screp((slice(None), b)),
                 in_=V(sc((slice(None), slice(None), slice(b, b + 1))), lambda a: a.to_broadcast([128, 8, 128])))
        mwv = V(I["mod_w"](), lambda a: a.rearrange("(k p) n -> p k n", p=128))
        ipg = 0
        for q in range(6):
            m = mw[q % 2]
            P.dma("sp", m(), V(mwv, lambda a: a[:, :, q * D:(q + 1) * D]))
            pmq = pm[q % 2]
            for j in range(8):
                for k in range(8):
                    P.op("pe", "matmul", out=pmq((slice(None), j)), lhsT=m((slice(None), k, slice(j * 128, (j + 1) * 128))),
                         rhs=sc((slice(None), k)), start=(k == 0), stop=(k == 7))
            for j in range(8):
                P.op("dve", "tensor_scalar", out=G.modT((slice(None), 8 * q + j)), in0=pmq((slice(None), j)),
                     scalar1=mbT((slice(None), slice(8 * q + j, 8 * q + j + 1))), scalar2=None, op0=ALU.add)
            if q in (2, 5):
                qi = 0 if q == 2 else 1
                for b in range(nseq):
                    for hf in range(2):
                        pgq = pg[ipg % 2]
                        ipg += 1
                        for k in range(8):
                            P.op("pe", "matmul", out=pgq(), lhsT=screp((slice(None), b, k)),
                                 rhs=m((slice(None), k, slice(hf * 512, (hf + 1) * 512))), start=(k == 0), stop=(k == 7))
                        gs_ = gst[ipg % 2]
                        P.op("dve", "tensor_tensor", out=gs_(),
                             in0=pgq(), in1=mbrow((slice(None), qi, slice(hf * 512, (hf + 1) * 512))), op=ALU.add)
                        P.dma("sp", G.gts((b, qi, slice(None), slice(hf * 512, (hf + 1) * 512))), gs_())
        for (dst, nw, q) in ((G.scale1, n1, 1), (G.scale2, n2, 4)):
            P.op("dve", "tensor_scalar", out=dst(), in0=G.modT((slice(None), slice(8 * q, 8 * q + 8))),
                 scalar1=1.0, scalar2=None, op0=ALU.add)
            P.op("dve", "tensor_tensor", out=dst(), in0=dst(),
                 in1=V(nw(), lambda a: a.unsqueeze(2).to_broadcast([128, 8, 5])), op=ALU.mult)
        P.barrier()


def phaseA(G):
    P, I, nseq, nc = G.P, G.I, G.nseq, G.nc
    with ExitStack() as es:
        def sbl(name, shape, dt=F32, n=1):
            return Buf(es.enter_context(SBT(nc, name, list(shape), dt)), n)

        def psl(name, shape, dt=F32, n=1):
            return Buf(es.enter_context(PST(nc, name, list(shape), dt)), n)

        w = G.w_in_bf
        NXT = 3
        xt = [sbl("xt%d" % i, [128, D]) for i in range(NXT)]
        xn = [sbl("xn%d" % i, [128, D], BF16) for i in range(2)]
        junk = sbl("junkA", [128, D], BF16)
        ss = [sbl("ss%d" % i, [128, 1]) for i in range(2)]
        rstd = [sbl("rstd%d" % i, [128, 1]) for i in range(2)]
        xm = [sbl("xm%d" % i, [128, 8, 512], BF16) for i in range(2)]
        stg_b = [sbl("stgb%d" % i, [128, 512], BF16) for i in range(4)]
        stg_f = [sbl("stgf%d" % i, [128, 512], F32) for i in range(2)]
        stg_d = [sbl("stgd%d" % i, [128, 4, 16], F32) for i in range(2)]
        tp = [psl("tpA%d" % i, [128, 8, 128], BF16) for i in range(2)]
        pmm = [psl("pmmA%d" % i, [128, 512]) for i in range(4)]
        pdt = [psl("pdtA%d" % i, [128, 4, 16]) for i in range(2)]
        st_ = {"it": 0, "imm": 0, "isb": 0, "isf": 0}
        groups = [(b, t0_, n_) for b in range(nseq) for (t0_, n_) in ([(0, 2)] + [(2 + 4 * g, 4) for g in range(4)])]

        def front(gi):
            b, tile0, ntile = groups[gi]
            bsel = 4 if tile0 == 0 else b
            xmg = xm[gi % 2]
            for il in range(ntile):
                t0 = (tile0 + il) * 128
                it = st_["it"]
                st_["it"] += 1
                x_, xn_, ss_, rs_, tp_ = xt[it % NXT], xn[it % 2], ss[it % 2], rstd[it % 2], tp[it % 2]
                P.dma("sp", x_(), I["xall"]((b, slice(t0, t0 + 128))))
                P.op("act", "activation", out=junk(), in_=x_(), func=AF.Square, accum_out=ss_())
                P.op("act", "activation", out=rs_(), in_=ss_(), func=AF.Sqrt, bias=G.epsc(), scale=1.0 / D)
                P.op("dve", "reciprocal", out=rs_(), in_=rs_())
                P.op("dve", "tensor_scalar", out=xn_(), in0=x_(), scalar1=rs_(), scalar2=None, op0=ALU.mult)
                yield
                for k in range(8):
                    P.op("pe", "transpose", out=tp_((slice(None), k)), in_=xn_((slice(None), slice(k * 128, (k + 1) * 128))),
                         identity=G.ident_b())
                for k in range(8):
                    o = xmg((slice(None), k, slice(il * 128, (il + 1) * 128)))
                    s1 = G.scale1((slice(None), k, slice(bsel, bsel + 1)))
                    s2 = G.modT((slice(None), k, slice(bsel, bsel + 1)))
                    if k % 2 == 0:
                        P.op("dve", "tensor_scalar", out=o, in0=tp_((slice(None), k)), scalar1=s1, scalar2=s2,
                             op0=ALU.mult, op1=ALU.add)
                    else:
                        P.op("act", "activation", out=o, in_=tp_((slice(None), k)), func=AF.Identity, bias=s2, scale=s1)
                yield
                yield

        def body(gi):
            b, tile0, ntile = groups[gi]
            xmg = xm[gi % 2]
            N = ntile * 128
            tsl = slice(tile0 * 128, tile0 * 128 + N)
            for j in range(27):
                pm_ = pmm[st_["imm"] % 4]
                st_["imm"] += 1
                for k in range(8):
                    P.op("pe", "matmul", out=pm_((slice(None), slice(0, N))), lhsT=w((slice(None), k, slice(j * 128, (j + 1) * 128))),
                         rhs=xmg((slice(None), k, slice(0, N))), start=(k == 0), stop=(k == 7))
                if 12 <= j <= 14:
                    st = stg_f[st_["isf"] % 2]
                    st_["isf"] += 1
                    P.op("dve", "tensor_copy", out=st((slice(None), slice(0, N))), in_=pm_((slice(None), slice(0, N))))
                    P.dma("sp", G.uTf((b, slice((j - 12) * 128, (j - 11) * 128), tsl)), st((slice(None), slice(0, N))))
                else:
                    st = stg_b[st_["isb"] % 4]
                    if st_["isb"] % 2 == 0:
                        P.op("act", "activation", out=st((slice(None), slice(0, N))), in_=pm_((slice(None), slice(0, N))),
                             func=AF.Copy)
                    else:
                        P.op("dve", "tensor_copy", out=st((slice(None), slice(0, N))), in_=pm_((slice(None), slice(0, N))))
                    st_["isb"] += 1
                    P.dma("sp", G.uT((b, slice(j * 128, (j + 1) * 128), tsl)), st((slice(None), slice(0, N))))
                yield
            pd_ = pdt[gi % 2]
            sd_ = stg_d[gi % 2]
            for il in range(ntile):
                for k in range(8):
                    P.op("pe", "matmul", out=pd_((slice(None), il)), lhsT=xmg((slice(None), k, slice(il * 128, (il + 1) * 128))),
                         rhs=w((slice(None), k, slice(3456, 3472))), start=(k == 0), stop=(k == 7))
            P.op("dve", "tensor_copy", out=sd_((slice(None), slice(0, ntile))), in_=pd_((slice(None), slice(0, ntile))))
            P.dma("sp", V(G.dtt((b, tsl)), lambda a: a.rearrange("(i p) c -> p i c", p=128)), sd_((slice(None), slice(0, ntile))))
            yield

        def rr(ga, gb, ratio=2):
            a_done = b_done = False
            while not (a_done and b_done):
                for _ in range(ratio):
                    if not a_done:
                        a_done = next(ga, "end") == "end"
                if not b_done:
                    b_done = next(gb, "end") == "end"

        for _ in front(0):
            pass
        for gi in range(len(groups)):
            rr(body(gi), front(gi + 1) if gi + 1 < len(groups) else iter(()), ratio=2)
        P.barrier()


def host_inputs(inputs, core, nseq=NSEQ):
    b0 = core * NSEQ
    x = inputs["x"][b0:b0 + nseq]
    ctx = inputs["ctx"][b0:b0 + nseq]
    m = {}
    m["xall"] = np.ascontiguousarray(np.concatenate([ctx, x], axis=1))
    cc = np.concatenate([inputs["c"][b0:b0 + NSEQ], inputs["c_ctx"][None]], axis=0)
    m["cT"] = np.ascontiguousarray(cc.reshape(5, 8, 128).transpose(2, 1, 0))
    m["mod_w"] = np.ascontiguousarray(inputs["mod_w"][0])
    m["mod_bT"] = np.ascontiguousarray(inputs["mod_b"][0].reshape(48, 128).T)
    m["mod_b"] = np.ascontiguousarray(inputs["mod_b"][0][None])
    m["norm1T"] = np.ascontiguousarray(inputs["norm1_w"][0].reshape(8, 128).T)
    m["norm2T"] = np.ascontiguousarray(inputs["norm2_w"][0].reshape(8, 128).T)
    m["w_in"] = np.ascontiguousarray(inputs["w_in"][0])
    m["ident"] = np.eye(128, dtype=np.float32)
    f32 = lambda a: np.ascontiguousarray(a, dtype=np.float32)
    colT = lambda v, n: f32(np.asarray(v).reshape(n, 128).T)
    m["tshiftT"] = f32(inputs["tshift_w"][0].reshape(3, 15, 128).transpose(2, 1, 0))
    m["w0T"] = f32(inputs["w0"][0].reshape(2, 4, 128).transpose(2, 1, 0))
    m["a0T"] = f32(inputs["a0"][0].reshape(2, 4, 128).transpose(2, 1, 0))
    m["kkT"] = colT(inputs["k_k"][0], 4)
    m["kaT"] = colT(inputs["k_a"][0], 4)
    m["rkT"] = colT(inputs["r_k"][0].reshape(-1), 4)
    m["lnwT"] = colT(inputs["lnx_w"][0], 4)
    m["lnbT"] = colT(inputs["lnx_b"][0], 4)
    m["w2s"] = f32(inputs["w2"][0].reshape(128, 512))
    m["a2s"] = f32(inputs["a2"][0].reshape(128, 512))
    m["g2"] = f32(inputs["g2"][0])
    bo = np.zeros((128, 128), np.float32); bo[:64, :64] = 1; bo[64:, 64:] = 1
    m["bones"] = bo
    p = np.arange(128)[:, None]; f = np.arange(128)[None, :]
    m["masks"] = f32(np.stack([p <= f, p < f, p >= f, p > f], axis=1))
    rm = np.ones((128, 512), np.float32); rm[:, ::128] = 0
    m["rmask"] = rm
    m["convT"] = f32(inputs["conv_w"][0].reshape(9, 8, 128).transpose(2, 1, 0))
    m["convbT"] = colT(inputs["conv_b"][0], 8)
    m["dtb"] = f32(np.broadcast_to(inputs["dt_bias"][0].reshape(1, 16), (128, 16)))
    m["alog"] = f32(np.broadcast_to(inputs["a_log"][0].reshape(1, 16), (128, 16)))
    m["dskipT"] = colT(np.repeat(inputs["d_skip"][0], 64), 4)
    m["gnwT"] = colT(inputs["gnorm_w"][0], 4)
    m["w_out"] = f32(inputs["w_out"][0])
    m["w_gu"] = f32(inputs["w_gu"][0])
    m["w_down"] = f32(inputs["w_down"][0])
    m["fnw"] = f32(inputs["final_norm_w"][None])
    return m


S_ = slice(None)
C0 = 0.6065306597126334
TBLOCKS = [(0, 256)] + [(256 + 512 * i, 512) for i in range(4)]
ORDER_F = list(range(NCH))
ORDER_B = [1, 0] + list(range(NCH - 1, 1, -1))


USE_F32R = False


def FR(v):
    if not USE_F32R:
        return v
    return View(v.ap.bitcast(mybir.dt.float32r), v.trks)


def sl(a, n):
    return slice(a, a + n)


def rw_setup(G, es):
    P, I, nc = G.P, G.I, G.nc

    def sbl(name, shape, dt=F32, n=1):
        return Buf(es.enter_context(SBT(nc, "rw_" + name, list(shape), dt)), n)

    R = Ctx()
    G.R = R
    R.tsh = sbl("tshT", [128, 15, 3])
    R.w0 = sbl("w0T", [128, 4, 2])
    R.a0 = sbl("a0T", [128, 4, 2])
    R.kk = sbl("kkT", [128, 4])
    R.ka = sbl("kaT", [128, 4])
    R.omka = sbl("omkaT", [128, 4])
    R.rk = sbl("rkT", [128, 4])
    R.lnw = sbl("lnwT", [128, 4])
    R.lnb = sbl("lnbT", [128, 4])
    R.w2 = sbl("w2b", [128, 512], BF16)
    R.a2 = sbl("a2b", [128, 512], BF16)
    R.g2 = sbl("g2b", [128, 512], BF16)
    R.bones = sbl("bonesb", [128, 128], BF16)
    R.masks_f = sbl("masksf", [128, 4, 128])
    R.masks_b = sbl("masksb", [128, 4, 128], BF16)
    R.m4 = sbl("m4", [128, 2, 4, 128], BF16)
    R.rmask = sbl("rmask", [128, 512])
    for dst, nm in ((R.tsh, "tshiftT"), (R.w0, "w0T"), (R.a0, "a0T"), (R.kk, "kkT"), (R.ka, "kaT"), (R.rk, "rkT"),
                    (R.lnw, "lnwT"), (R.lnb, "lnbT"), (R.masks_f, "masks"), (R.rmask, "rmask")):
        P.dma("sp", dst(), I[nm]())
    for dst, nm in ((R.w2, "w2s"), (R.a2, "a2s"), (R.g2, "g2"), (R.bones, "bones")):
        P.dma("pool", dst(), I[nm]())
    P.op("dve", "tensor_scalar", out=R.omka(), in0=R.ka(), scalar1=-1.0, scalar2=1.0, op0=ALU.mult, op1=ALU.add)
    P.op("dve", "tensor_copy", out=R.masks_b(), in_=R.masks_f())
    for d, (strict, incl) in enumerate(((1, 0), (3, 2))):
        for q, mi in enumerate((strict, incl, strict, incl)):
            P.op("dve", "tensor_copy", out=R.m4((S_, d, q)), in_=R.masks_f((S_, mi)))
    R.diag = sbl("diag", [128, 12, 3, 128], BF16)
    for tl in range(12):
        for tap in range(3):
            P.op("dve", "tensor_scalar", out=R.diag((S_, tl, tap)), in0=G.ident_f(), scalar1=R.tsh((S_, tl, sl(tap, 1))),
                 scalar2=None, op0=ALU.mult)
    nseq = G.nseq
    R.rwF = G.dscr("rwF", [nseq, 2, 4, 512, T], BF16)
    R.rwT = G.dscr("rwT", [nseq, 5, T, 512], BF16)
    G.yTs = G.dscr("yTs", [nseq, 8 * 128, SEQ], BF16)
    R.gc = sbl("gc", [128, 4, 2, NCH])
    R.yacc = sbl("yacc", [128, 16, 512])


def phaseB1a(G, b):
    P, I, nc, R = G.P, G.I, G.nc, G.R
    with ExitStack() as es:
        def sbl(name, shape, dt=F32, n=1):
            return Buf(es.enter_context(SBT(nc, name, list(shape), dt)), n)

        def psl(name, shape, dt=F32, n=1):
            return Buf(es.enter_context(PST(nc, name, list(shape), dt)), n)

        uin_b = [sbl("uinb%d" % i, [128, 514], BF16) for i in range(6)]
        uin_f2 = [[sbl("uinf%d_%d" % (k, i), [128, 514]) for i in range(3)] for k in range(2)]
        usl = [sbl("usl%d" % i, [128, 512]) for i in range(3)]
        th = sbl("th", [128, 512], BF16)
        adb = sbl("adb", [128, 512], BF16)
        sg = sbl("sg", [128, 512], BF16)
        us_r2 = [sbl("us_r%d" % i, [128, 512]) for i in range(2)]
        us_k2 = [sbl("us_k%d" % i, [128, 512]) for i in range(2)]
        us_v2 = [sbl("us_v%d" % i, [128, 512]) for i in range(2)]
        sigw2 = [[sbl("sigw%d_%d" % (i, d), [128, 512]) for d in range(2)] for i in range(2)]
        aa2 = [[sbl("aa%d_%d" % (i, d), [128, 512]) for d in range(2)] for i in range(2)]
        kkn2 = [sbl("kkn%d" % i, [128, 512]) for i in range(2)]
        kkk, rn = sbl("kkk", [128, 512]), sbl("rn", [128, 512])
        sq = sbl("sq", [128, 512], BF16)
        rkb = sbl("rkb", [128, 512], BF16)
        Gs, Gi, Ge = (sbl(n, [128, 512]) for n in ("Gs", "Gi", "Ge"))
        eGi, eGe, eGn = (sbl(n, [128, 512]) for n in ("eGi", "eGe", "eGn"))
        kd, beta, tmp, tmp2 = (sbl(n, [128, 512]) for n in ("kd", "beta", "tmpb", "tmpc"))
        outs = [sbl("o%d" % i, [128, 512], BF16) for i in range(4)]
        FM = sbl("FM", [128, 5, 4, 512], BF16, n=5)
        tstg = [sbl("tstg%d" % i, [128, 512], BF16) for i in range(2)]
        psh = [psl("pshB%d" % i, [128, 512]) for i in range(3)]
        pl = [psl("plB%d" % i, [128, 512]) for i in range(2)]
        pn = [psl("pnB%d" % i, [128, 512]) for i in range(1)] * 2
        ptr = [psl("ptrB%d" % i, [128, 4, 128], BF16) for i in range(2)]
        st_ = {"io": 0, "itr": 0}
        v3 = lambda a: a.rearrange("p (c t) -> p c t", t=128)

        def geom(bi):
            t0, N = TBLOCKS[bi]
            seg0, seg1 = (0, CTX) if t0 < CTX else (CTX, T)
            return t0, N, seg0, seg1

        def load(bi, dst, src_rows, dram):
            t0, N, seg0, seg1 = geom(bi)
            lo = max(t0 - 1, seg0)
            hi = min(t0 + N + 1, seg1)
            c_lo = lo - (t0 - 1)
            if c_lo > 0:
                P.op("pool", "memset", ap=dst((S_, sl(0, c_lo))), constant=0.0)
            c_hi = c_lo + (hi - lo)
            if c_hi < N + 2:
                P.op("pool", "memset", ap=dst((S_, sl(c_hi, N + 2 - c_hi))), constant=0.0)
            P.dma("sp", dst((S_, sl(c_lo, hi - lo))), dram((b, src_rows, slice(lo, hi))))

        def loadj(bi, j):
            for q3 in range(3):
                load(bi, uin_b[3 * (j % 2) + q3], sl((4 * q3 + j) * 128, 128), G.uT)

        def loadf(bi):
            for jj in range(3):
                load(bi, uin_f2[bi % 2][jj], sl(jj * 128, 128), G.uTf)

        def front(bi, j):
            t0, N, _, _ = geom(bi)
            n_ = sl(0, N)
            jc = sl(j * 128, 128)
            jp = j % 2
            us_r, us_k, us_v = us_r2[jp], us_k2[jp], us_v2[jp]
            sigw, aa, kkn = sigw2[jp], aa2[jp], kkn2[jp]
            for q3, dst in enumerate((us_r, us_k, us_v)):
                src_ = uin_b[3 * jp + q3]
                for tap in range(3):
                    P.op("pe", "matmul", out=psh[q3]((S_, n_)), lhsT=R.diag((S_, 4 * q3 + j, tap)), rhs=src_((S_, sl(tap, N))),
                         start=(tap == 0), stop=(tap == 2))
                if q3 == 1:
                    P.op("dve", "tensor_copy", out=dst((S_, n_)), in_=psh[q3]((S_, n_)))
                else:
                    P.op("act", "activation", out=dst((S_, n_)), in_=psh[q3]((S_, n_)), func=AF.Copy)
                yield
            if j + 2 < 4:
                loadj(bi, j + 2)
            P.op("act", "activation", out=kkk((S_, n_)), in_=us_k((S_, n_)), func=AF.Identity, scale=R.kk((S_, sl(j, 1))))
            P.op("act", "activation", out=sq((S_, n_)), in_=kkk((S_, n_)), func=AF.Square)
            P.op("pe", "matmul", out=pn[0]((S_, n_)), lhsT=R.bones(), rhs=sq((S_, n_)), start=True, stop=True)
            yield
            for d in range(2):
                pr = sl(d * 64, 64)
                P.op("pe", "matmul", out=pl[0]((S_, n_)), lhsT=R.w2((pr, jc)), rhs=th((pr, n_)), start=True, stop=True)
                P.op("pe", "matmul", out=pl[1]((S_, n_)), lhsT=R.a2((pr, jc)), rhs=adb((pr, n_)), start=True, stop=True)
                P.op("act", "activation", out=sigw[d]((S_, n_)), in_=pl[0]((S_, n_)), func=AF.Sigmoid,
                     bias=R.w0((S_, j, sl(d, 1))))
                P.op("act", "activation", out=aa[d]((S_, n_)), in_=pl[1]((S_, n_)), func=AF.Sigmoid,
                     bias=R.a0((S_, j, sl(d, 1))))
                yield
            P.op("dve", "tensor_scalar", out=rn((S_, n_)), in0=pn[0]((S_, n_)), scalar1=1e-19, scalar2=None, op0=ALU.max)
            yield
            P.op("act", "activation", out=rn((S_, n_)), in_=rn((S_, n_)), func=AF.Ln)
            P.op("act", "activation", out=rn((S_, n_)), in_=rn((S_, n_)), func=AF.Exp, scale=-0.5)
            yield
            yield
            P.op("dve", "tensor_tensor", out=kkn((S_, n_)), in0=kkk((S_, n_)), in1=rn((S_, n_)), op=ALU.mult)
            yield

        def tail(bi, j):
            t0, N, _, _ = geom(bi)
            n_ = sl(0, N)
            nchk = N // 128
            lat0 = t0 - CTX
            jc = sl(j * 128, 128)
            jp = j % 2
            us_r, us_k, us_v = us_r2[jp], us_k2[jp], us_v2[jp]
            sigw, aa, kkn = sigw2[jp], aa2[jp], kkn2[jp]
            if t0 >= CTX:
                lt = sl(lat0, N)
                P.op("dve", "scalar_tensor_tensor", out=rkb((S_, n_)), in0=us_r((S_, n_)), scalar=R.rk((S_, sl(j, 1))),
                     in1=us_k((S_, n_)), op0=ALU.mult, op1=ALU.mult)
                P.op("pe", "matmul", out=pn[1]((S_, n_)), lhsT=R.bones(), rhs=rkb((S_, n_)), start=True, stop=True)
                P.op("dve", "tensor_tensor", out=R.bonusT((S_, j, lt)), in0=pn[1]((S_, n_)), in1=us_v((S_, n_)), op=ALU.mult)
                P.op("pe", "matmul", out=pn[1]((S_, n_)), lhsT=R.g2((S_, jc)), rhs=sg((S_, n_)), start=True, stop=True)
                P.op("act", "activation", out=R.gT((S_, j, lt)), in_=pn[1]((S_, n_)), func=AF.Copy)
            P.op("act", "activation", out=FM((S_, 4, j, n_), k=4), in_=us_v((S_, n_)), func=AF.Copy)
            yield
            for d in range(2):
                sw = sigw[d]
                P.op("dve", "tensor_tensor_scan", out=Gs((S_, n_)), data0=R.rmask((S_, n_)), data1=sw((S_, n_)),
                     initial=0.0, op0=ALU.mult, op1=ALU.add)
                if d == 0:
                    Gi_ = Gs
                else:
                    tot = V(Gs((S_, n_)), lambda a: v3(a)[:, :, 127:128].to_broadcast([128, nchk, 128]))
                    P.op("dve", "tensor_tensor", out=V(Ge((S_, n_)), v3), in0=tot, in1=V(Gs((S_, n_)), v3), op=ALU.subtract)
                    P.op("dve", "tensor_tensor", out=Gi((S_, n_)), in0=Ge((S_, n_)), in1=sw((S_, n_)), op=ALU.add)
                    Gi_ = Gi
                if d == 0:
                    P.op("dve", "tensor_tensor", out=Ge((S_, n_)), in0=Gi_((S_, n_)), in1=sw((S_, n_)), op=ALU.subtract)
                yield
                P.op("act", "activation", out=tmp((S_, n_)), in_=aa[d]((S_, n_)), func=AF.Identity,
                     scale=R.ka((S_, sl(j, 1))), bias=R.omka((S_, sl(j, 1))))
                P.op("act", "activation", out=eGi((S_, n_)), in_=Gi_((S_, n_)), func=AF.Exp, scale=-C0)
                P.op("act", "activation", out=eGe((S_, n_)), in_=Ge((S_, n_)), func=AF.Exp, scale=-C0)
                P.op("act", "activation", out=eGn((S_, n_)), in_=Gi_((S_, n_)), func=AF.Exp, scale=C0)
                P.op("dve", "tensor_tensor", out=kd((S_, n_)), in0=tmp((S_, n_)), in1=us_k((S_, n_)), op=ALU.mult)
                P.op("dve", "tensor_tensor", out=beta((S_, n_)), in0=kkn((S_, n_)), in1=aa[d]((S_, n_)), op=ALU.mult)
                yield
                c0i = t0 // 128
                col = 127 if d == 0 else 0
                P.op("dve", "tensor_copy", out=R.gc((S_, j, d, sl(c0i, nchk))),
                     in_=V(eGi((S_, n_)), lambda a: v3(a)[:, :, col]))
                oA, oR = outs[st_["io"] % 4], outs[(st_["io"] + 1) % 4]
                st_["io"] += 2
                oB, oK = FM((S_, 2 * d, j, n_), k=2 * d), FM((S_, 2 * d + 1, j, n_), k=2 * d + 1)
                P.op("dve", "scalar_tensor_tensor", out=oA((S_, n_)), in0=kkn((S_, n_)), scalar=-1.0, in1=eGe((S_, n_)),
                     op0=ALU.mult, op1=ALU.mult)
                P.op("dve", "tensor_tensor", out=oR((S_, n_)), in0=us_r((S_, n_)), in1=eGi((S_, n_)), op=ALU.mult)
                yield
                P.op("dve", "tensor_tensor", out=oB, in0=beta((S_, n_)), in1=eGn((S_, n_)), op=ALU.mult)
                P.op("dve", "tensor_tensor", out=oK, in0=kd((S_, n_)), in1=eGn((S_, n_)), op=ALU.mult)
                P.dma("sp", R.rwF((b, d, 0, jc, sl(t0, N))), oA((S_, n_)))
                P.dma("sp", R.rwF((b, d, 1, jc, sl(t0, N))), oR((S_, n_)))
                P.dma("sp", R.rwF((b, d, 2, jc, sl(t0, N))), oB)
                P.dma("sp", R.rwF((b, d, 3, jc, sl(t0, N))), oK)
                yield

        def lora_prefix(bi):
            t0, N, _, _ = geom(bi)
            n_ = sl(0, N)
            uf = uin_f2[bi % 2]
            for jj in range(3):
                j_ = 12 + jj
                dst, src_ = usl[jj], uf[jj]
                P.op("act", "activation", out=dst((S_, n_)), in_=src_((S_, sl(1, N))), func=AF.Identity,
                     scale=R.tsh((S_, j_, sl(1, 1))))
                P.op("dve", "scalar_tensor_tensor", out=dst((S_, n_)), in0=src_((S_, sl(0, N))),
                     scalar=R.tsh((S_, j_, sl(0, 1))), in1=dst((S_, n_)), op0=ALU.mult, op1=ALU.add)
                P.op("dve", "scalar_tensor_tensor", out=dst((S_, n_)), in0=src_((S_, sl(2, N))),
                     scalar=R.tsh((S_, j_, sl(2, 1))), in1=dst((S_, n_)), op0=ALU.mult, op1=ALU.add)
            P.op("act", "activation", out=th((S_, n_)), in_=usl[0]((S_, n_)), func=AF.Tanh)
            P.op("act", "activation", out=sg((S_, n_)), in_=usl[2]((S_, n_)), func=AF.Sigmoid)
            P.op("dve", "tensor_copy", out=adb((S_, n_)), in_=usl[1]((S_, n_)))

        def transposes(bi):
            t0, N, _, _ = geom(bi)
            for q in range(5):
                for c in range(N // 128):
                    pt = ptr[st_["itr"] % 2]
                    st = tstg[st_["itr"] % 2]
                    st_["itr"] += 1
                    for j in range(4):
                        P.op("pe", "transpose", out=pt((S_, j)), in_=FM((S_, q, j, sl(c * 128, 128)), k=q), identity=G.ident_b())
                    if st_["itr"] % 2:
                        P.op("act", "activation", out=st(), in_=V(pt(), lambda a: a.rearrange("p a b -> p (a b)")), func=AF.Copy)
                    else:
                        P.op("dve", "tensor_copy", out=st(), in_=V(pt(), lambda a: a.rearrange("p a b -> p (a b)")))
                    P.dma("sp", R.rwT((b, q, sl(t0 + c * 128, 128))), st())

        def rr(ga, gb):
            a_done = b_done = False
            while not (a_done and b_done):
                if not a_done:
                    a_done = next(ga, "end") == "end"
                if not b_done:
                    b_done = next(gb, "end") == "end"

        nb = len(TBLOCKS)
        loadf(0)
        loadj(0, 0)
        loadj(0, 1)
        lora_prefix(0)
        for _ in front(0, 0):
            pass
        for bi in range(nb):
            for j in range(4):
                rr(tail(bi, j), front(bi, j + 1) if j + 1 < 4 else iter(()))
            if bi + 1 < nb:
                loadf(bi + 1)
                loadj(bi + 1, 0)
                loadj(bi + 1, 1)
            transposes(bi)
            if bi + 1 < nb:
                lora_prefix(bi + 1)
                for _ in front(bi + 1, 0):
                    pass
        P.barrier()


def phaseB1b(G, b):
    P, I, nc, R = G.P, G.I, G.nc, G.R
    with ExitStack() as es:
        def sbl(name, shape, dt=F32, n=1):
            return Buf(es.enter_context(SBT(nc, name, list(shape), dt)), n)

        def psl(name, shape, dt=F32, n=1):
            return Buf(es.enter_context(PST(nc, name, list(shape), dt)), n)

        NB = 3
        cf = [sbl("cf%d" % i, [128, 4, 4, 128], BF16) for i in range(NB)]
        ct = [sbl("ct%d" % i, [128, 3, 512], BF16) for i in range(NB)]
        M1 = [sbl("M1_%d" % i, [128, 8, 512], BF16, n=8) for i in range(2)]
        TB = [[sbl("TB%d_%d" % (i, h), [128, 384], F32) for h in range(8)] for i in range(2)]
        TT = [sbl("TT%d" % i, [128, 8, 128], BF16, n=8) for i in range(2)]
        H = sbl("Hst", [128, 4, 64])
        Hb = sbl("Hbf", [128, 4, 64], BF16)
        Wsb = sbl("Wsb", [128, 512], BF16)
        Usb = sbl("Usb", [128, 512], BF16)
        pA = [psl("pA%d" % i, [128, 512]) for i in range(2)]
        pB = [psl("pB%d" % i, [128, 512]) for i in range(3)]
        pWU, pY = psl("pWU", [128, 512]), psl("pY", [128, 512])
        pH = psl("pH", [128, 4, 64])
        cnt = {"a": 0, "b": 0}
        def t_phase(d, c, slot):
            cf_, ct_ = cf[slot % NB], ct[slot % NB]
            M1_, TT_ = M1[slot % 2], TT[slot % 2]
            tsl = sl(c * 128, 128)
            for q in range(4):
                P.dma("sp", cf_((S_, S_, q)), V(R.rwF((b, d, q, S_, tsl)), lambda a: a.rearrange("(j p) t -> p j t", p=128)))
            P.dma("sp", ct_((S_, 0)), R.rwT((b, 2 * d, tsl)))
            P.dma("sp", ct_((S_, 1)), R.rwT((b, 2 * d + 1, tsl)))
            P.dma("sp", ct_((S_, 2)), R.rwT((b, 4, tsl)))
            yield
            mN = 3 if d == 0 else 1
            mT = 1 if d == 0 else 3
            cur = [TB[0][h] for h in range(8)]
            nxt = [TB[1][h] for h in range(8)]
            for h in range(8):
                j, pb = h // 2, sl((h % 2) * 64, 64)
                A_, B_, K_ = cf_((pb, j, 0)), cf_((pb, j, 2)), cf_((pb, j, 3))
                AR = cf_((pb, j, sl(0, 2)))
                p1 = pA[cnt["a"] % 2]
                cnt["a"] += 1
                P.op("pe", "matmul", out=p1((S_, sl(0, 256))), lhsT=B_, rhs=AR, start=True, stop=True)
                P.op("pe", "matmul", out=p1((S_, sl(256, 256))), lhsT=K_, rhs=AR, start=True, stop=True)
                p2 = pB[cnt["b"] % 3]
                cnt["b"] += 1
                P.op("pe", "matmul", out=p2((S_, sl(0, 128))), lhsT=A_, rhs=B_, start=True, stop=True)
                P.op("dve", "tensor_tensor", out=M1_((S_, h), k=h), in0=p1(),
                     in1=V(R.m4((S_, d)), lambda a: a.rearrange("p a b -> p (a b)")), op=ALU.mult)
                P.op("dve", "tensor_tensor", out=FR(cur[h]((S_, sl(128, 128)))), in0=p1((S_, sl(0, 128))),
                     in1=R.masks_f((S_, mT)), op=ALU.mult)
                P.op("dve", "tensor_tensor", out=FR(cur[h]((S_, sl(256, 128)))), in0=p2((S_, sl(0, 128))),
                     in1=R.masks_f((S_, mN)), op=ALU.mult)
                if h % 4 == 3:
                    yield
            for h in range(8):
                p2 = pB[cnt["b"] % 3]
                cnt["b"] += 1
                c_, n_ = cur[h], nxt[h]
                P.op("pe", "matmul", out=p2((S_, sl(128, 128))), lhsT=FR(c_((S_, sl(256, 128)))), rhs=FR(c_((S_, sl(128, 128)))),
                     start=True, stop=True)
                P.op("pe", "matmul", out=p2((S_, sl(256, 128))), lhsT=FR(c_((S_, sl(128, 128)))), rhs=FR(c_((S_, sl(256, 128)))),
                     start=True, stop=True)
                P.op("dve", "tensor_tensor", out=FR(n_((S_, sl(0, 128)))), in0=c_((S_, sl(128, 128))), in1=G.ident_f(), op=ALU.add)
                P.op("act", "activation", out=FR(n_((S_, sl(128, 256)))), in_=p2((S_, sl(128, 256))), func=AF.Copy)
                if h % 4 == 3:
                    yield
            cur, nxt = nxt, cur
            for k in range(1, 7):
                last = k == 6
                for h in range(8):
                    p2 = pB[cnt["b"] % 3]
                    cnt["b"] += 1
                    c_, n_ = cur[h], nxt[h]
                    w_ = 128 if (last or k == 5) else 256
                    P.op("pe", "matmul", out=p2((S_, sl(0, w_))), lhsT=FR(c_((S_, sl(256, 128)))), rhs=FR(c_((S_, sl(0, w_)))),
                         start=True, stop=True)
                    if not last:
                        P.op("pe", "matmul", out=p2((S_, sl(256, 128))), lhsT=FR(c_((S_, sl(128, 128)))), rhs=FR(c_((S_, sl(256, 128)))),
                             start=True, stop=True)
                        P.op("dve", "tensor_tensor", out=FR(n_((S_, sl(0, 128)))), in0=p2((S_, sl(0, 128))),
                             in1=c_((S_, sl(0, 128))), op=ALU.add)
                        if k == 5:
                            P.op("act", "activation", out=FR(n_((S_, sl(256, 128)))), in_=p2((S_, sl(256, 128))), func=AF.Copy)
                        else:
                            P.op("act", "activation", out=FR(n_((S_, sl(128, 256)))), in_=p2((S_, sl(128, 256))), func=AF.Copy)
                    else:
                        P.op("dve", "tensor_tensor", out=TT_((S_, h), k=h), in0=p2((S_, sl(0, 128))),
                             in1=c_((S_, sl(0, 128))), op=ALU.add)
                    if h % 4 == 3:
                        yield
                cur, nxt = nxt, cur

        def s_phase(d, c, slot):
            cf_, ct_ = cf[slot % NB], ct[slot % NB]
            M1_, TT_ = M1[slot % 2], TT[slot % 2]
            latent = c >= 2
            for h in range(8):
                j, pb, hc = h // 2, sl((h % 2) * 64, 64), sl(h * 64, 64)
                P.op("pe", "matmul", out=pWU((S_, hc)), lhsT=cf_((pb, j, 0)), rhs=Hb((pb, j)), start=True, stop=False)
                P.op("pe", "matmul", out=pWU((S_, hc)), lhsT=M1_((S_, h, sl(256, 128)), k=h), rhs=ct_((S_, 2, hc)),
                     start=False, stop=True)
            P.op("act", "activation", out=Wsb(), in_=pWU(), func=AF.Copy)
            yield
            for h in range(8):
                hc = sl(h * 64, 64)
                P.op("pe", "matmul", out=pWU((S_, hc)), lhsT=TT_((S_, h), k=h), rhs=Wsb((S_, hc)), start=True, stop=True)
            P.op("dve", "tensor_copy", out=Usb(), in_=pWU())
            yield
            if latent:
                for h in range(8):
                    j, pb, hc = h // 2, sl((h % 2) * 64, 64), sl(h * 64, 64)
                    P.op("pe", "matmul", out=pY((S_, hc)), lhsT=cf_((pb, j, 1)), rhs=Hb((pb, j)), start=True, stop=False)
                    P.op("pe", "matmul", out=pY((S_, hc)), lhsT=M1_((S_, h, sl(128, 128)), k=h), rhs=Usb((S_, hc)),
                         start=False, stop=False)
                    P.op("pe", "matmul", out=pY((S_, hc)), lhsT=M1_((S_, h, sl(384, 128)), k=h), rhs=ct_((S_, 2, hc)),
                         start=False, stop=True)
                if d == 0:
                    P.op("act", "activation", out=R.yacc((S_, c - 2)), in_=pY(), func=AF.Copy)
                else:
                    P.op("dve", "tensor_tensor", out=R.yacc((S_, c - 2)), in0=pY(), in1=R.yacc((S_, c - 2)), op=ALU.add)
            for h in range(8):
                j, pb, hc = h // 2, sl((h % 2) * 64, 64), sl(h * 64, 64)
                P.op("pe", "matmul", out=pH((pb, j)), lhsT=ct_((S_, 0, hc)), rhs=Usb((S_, hc)), start=True, stop=False)
                P.op("pe", "matmul", out=pH((pb, j)), lhsT=ct_((S_, 1, hc)), rhs=ct_((S_, 2, hc)), start=False, stop=True)
            P.op("dve", "tensor_tensor", out=H(), in0=pH(), in1=H(), op=ALU.add)
            for j in range(4):
                P.op("act", "activation", out=H((S_, j)), in_=H((S_, j)), func=AF.Identity, scale=R.gc((S_, j, d, sl(c, 1))))
            P.op("act", "activation", out=Hb(), in_=H(), func=AF.Copy)
            yield

        work = [(d, c) for d in range(2) for c in (ORDER_F if d == 0 else ORDER_B)]
        for _ in t_phase(work[0][0], work[0][1], 0):
            pass
        for i, (d, c) in enumerate(work):
            if c == (ORDER_F if d == 0 else ORDER_B)[0]:
                P.op("dve", "memset", ap=H(), constant=0.0)
                P.op("dve", "memset", ap=Hb(), constant=0.0)
            gs = s_phase(d, c, i)
            gt = t_phase(work[i + 1][0], work[i + 1][1], i + 1) if i + 1 < len(work) else iter(())
            tn = 0
            s_done = False
            while True:
                t_done = next(gt, "end") == "end"
                tn += 1
                if not s_done and (tn % 5 == 2 or t_done):
                    s_done = next(gs, "end") == "end"
                if t_done:
                    while not s_done:
                        s_done = next(gs, "end") == "end"
                    break
        P.barrier()


def phaseB1c(G, b):
    P, I, nc, R = G.P, G.I, G.nc, G.R
    with ExitStack() as es:
        def sbl(name, shape, dt=F32, n=1):
            return Buf(es.enter_context(SBT(nc, name, list(shape), dt)), n)

        def psl(name, shape, dt=F32, n=1):
            return Buf(es.enter_context(PST(nc, name, list(shape), dt)), n)

        sq = sbl("c_sq", [128, 512])
        s1, s2, mu, var = (sbl(n, [128, 8]) for n in ("c_s1", "c_s2", "c_mu", "c_var"))
        yc = sbl("c_yc", [128, 512])
        yn = [sbl("c_yn%d" % i, [128, 512], BF16) for i in range(2)]
        t1 = sbl("c_t1", [128, 4, 128])
        yob = [sbl("c_yob%d" % i, [128, 4, 128], BF16) for i in range(2)]
        lneps = sbl("c_eps", [128, 1])
        P.op("dve", "memset", ap=lneps(), constant=RW_LN_EPS)
        pt = [psl("c_pt%d" % i, [128, 4, 128], BF16) for i in range(2)]
        v3 = lambda a: a.rearrange("p (h n) -> p h n", n=64)
        for c in range(16):
            y = R.yacc((S_, c))
            P.op("dve", "tensor_reduce", out=s1(), in_=V(y, v3), axis=AX.X, op=ALU.add)
            P.op("act", "activation", out=sq(), in_=y, func=AF.Square)
            P.op("dve", "tensor_reduce", out=s2(), in_=V(sq(), v3), axis=AX.X, op=ALU.add)
            P.op("dve", "tensor_scalar", out=mu(), in0=s1(), scalar1=1.0 / 64, scalar2=None, op0=ALU.mult)
            P.op("dve", "tensor_tensor", out=var(), in0=mu(), in1=mu(), op=ALU.mult)
            P.op("dve", "scalar_tensor_tensor", out=var(), in0=s2(), scalar=1.0 / 64, in1=var(), op0=ALU.mult, op1=ALU.subtract)
            P.op("act", "activation", out=var(), in_=var(), func=AF.Sqrt, bias=lneps())
            P.op("dve", "reciprocal", out=var(), in_=var())
            bc = lambda a: a.unsqueeze(2).to_broadcast([128, 8, 64])
            P.op("dve", "tensor_tensor", out=V(yc(), v3), in0=V(y, v3), in1=V(mu(), bc), op=ALU.subtract)
            yn_ = yn[c % 2]
            P.op("dve", "tensor_tensor", out=V(yn_(), v3), in0=V(yc(), v3), in1=V(var(), bc), op=ALU.mult)
            pt_ = pt[c % 2]
            for j in range(4):
                P.op("pe", "transpose", out=pt_((S_, j)), in_=yn_((S_, sl(j * 128, 128))), identity=G.ident_b())
            tk = sl(c * 128, 128)
            for j in range(4):
                P.op("act", "activation", out=t1((S_, j)), in_=pt_((S_, j)), func=AF.Identity, scale=R.lnw((S_, sl(j, 1))),
                     bias=R.lnb((S_, sl(j, 1))))
            P.op("dve", "tensor_tensor", out=t1(), in0=t1(), in1=R.bonusT((S_, S_, tk)), op=ALU.add)
            yo_ = yob[c % 2]
            P.op("dve", "tensor_tensor", out=yo_(), in0=t1(), in1=R.gT((S_, S_, tk)), op=ALU.mult)
            P.dma("sp", V(G.yTs((b, sl(0, 512), tk)), lambda a: a.rearrange("(j p) t -> p j t", p=128)), yo_())
        P.barrier()


def mb_setup(G, es):
    P, I, nc = G.P, G.I, G.nc

    def sbl(name, shape, dt=F32, n=1):
        return Buf(es.enter_context(SBT(nc, "mb_" + name, list(shape), dt)), n)

    M = Ctx()
    G.M = M
    M.conv = sbl("conv", [128, 8, 9])
    M.convb = sbl("convb", [128, 8])
    M.dtb = sbl("dtb", [128, 16])
    M.negA = sbl("negA", [128, 16])
    M.dskip = sbl("dskip", [128, 4])
    M.gnw = sbl("gnw", [128, 4])
    M.ones_f = sbl("ones_f", [128, 128])
    M.ones_b = sbl("ones_b", [128, 128], BF16)
    M.onec = sbl("onec", [128, 1])
    for dst, nm in ((M.conv, "convT"), (M.convb, "convbT"), (M.dtb, "dtb"), (M.negA, "alog"), (M.dskip, "dskipT"),
                    (M.gnw, "gnwT")):
        P.dma("sp", dst(), I[nm]())
    P.op("act", "activation", out=M.negA(), in_=M.negA(), func=AF.Exp)
    P.op("dve", "tensor_scalar", out=M.negA(), in0=M.negA(), scalar1=-1.0, scalar2=None, op0=ALU.mult)
    P.op("dve", "memset", ap=M.ones_f(), constant=1.0)
    P.op("dve", "memset", ap=M.ones_b(), constant=1.0)
    P.op("dve", "memset", ap=M.onec(), constant=1.0)
    nseq = G.nseq
    M.mF = G.dscr("mF", [nseq, 4 * 128, T], BF16)
    M.mT = G.dscr("mT", [nseq, T, 768], BF16)
    M.yacc = G.R.yacc


def phaseB2a(G, b):
    P, I, nc, M = G.P, G.I, G.nc, G.M
    with ExitStack() as es:
        def sbl(name, shape, dt=F32, n=1):
            return Buf(es.enter_context(SBT(nc, name, list(shape), dt)), n)

        def psl(name, shape, dt=F32, n=1):
            return Buf(es.enter_context(PST(nc, name, list(shape), dt)), n)

        HW = 65
        uin = [sbl("m_uin%d" % i, [128, 512 + 2 * HW], BF16) for i in range(3)]
        pconv = [psl("m_pconv%d" % i, [128, 512]) for i in range(2)]
        zin = [sbl("m_zin%d" % i, [128, 512], BF16) for i in range(4)]
        FMm = sbl("m_FM", [128, 8, 512], BF16, n=8)
        ptr = [psl("m_ptr%d" % i, [128, 6, 128], BF16) for i in range(2)]
        tst = [sbl("m_tst%d" % i, [128, 768], BF16) for i in range(2)]
        iu = 0
        itr = 0
        work = [(bi, i) for bi in range(len(TBLOCKS)) for i in range(8)]

        def issue_load(wi):
            bi, i = work[wi]
            t0, N = TBLOCKS[bi]
            seg0, seg1 = (0, CTX) if t0 < CTX else (CTX, T)
            u_ = uin[wi % 3]
            lo, hi = max(t0 - HW, seg0), min(t0 + N + HW, seg1)
            c_lo = lo - (t0 - HW)
            if c_lo > 0:
                P.op("pool", "memset", ap=u_((S_, sl(0, c_lo))), constant=0.0)
            c_hi = c_lo + hi - lo
            if c_hi < N + 2 * HW:
                P.op("pool", "memset", ap=u_((S_, sl(c_hi, N + 2 * HW - c_hi))), constant=0.0)
            P.dma("sp", u_((S_, sl(c_lo, hi - lo))), G.uT((b, sl((19 + i) * 128, 128), slice(lo, hi))))

        issue_load(0)
        issue_load(1)
        wi = 0
        for bi_, (t0, N) in enumerate(TBLOCKS):
            seg0, seg1 = (0, CTX) if t0 < CTX else (CTX, T)
            n_ = sl(0, N)
            lat = t0 >= CTX
            if lat:
                for i in range(4):
                    P.dma("sp", zin[i]((S_, n_)), G.uT((b, sl((15 + i) * 128, 128), sl(t0, N))))
            for i in range(8):
                u_ = uin[wi % 3]
                if wi + 2 < len(work):
                    issue_load(wi + 2)
                wi += 1
                iu += 1
                pc_ = pconv[iu % 2]
                if not lat:
                    taps = [(1, 1), (1, 0), (1, 2)]
                else:
                    taps = [(1, 1), (0, 1), (2, 1)] + [(kh, kw) for kh in range(3) for kw in (0, 2)]
                for ti, (kh, kw) in enumerate(taps):
                    dr, dc = (kh - 1 if lat else 0), kw - 1
                    lhs = M.cdiag((S_, i, kh * 3 + kw))
                    start = HW + 64 * dr + dc
                    if not lat or dc == 0:
                        src_v = u_((S_, sl(start, N)))
                        dst_v = pc_((S_, n_))
                    else:
                        v3 = lambda a: a.rearrange("p (r c) -> p r c", c=64)
                        cs = slice(1, 64) if dc == -1 else slice(0, 63)
                        src_v = V(u_((S_, sl(start, N))), lambda a: v3(a)[:, :, cs])
                        dst_v = V(pc_((S_, n_)), lambda a: v3(a)[:, :, cs])
                    P.op("pe", "matmul", out=dst_v, lhsT=lhs, rhs=src_v, start=(ti == 0), stop=(ti == len(taps) - 1))
                P.op("act", "activation", out=FMm((S_, i, n_), k=i), in_=pc_((S_, n_)), func=AF.Silu,
                     bias=M.convb((S_, sl(i, 1))))
                if i < 4 and lat:
                    P.op("pool", "tensor_copy", out=M.xsT((S_, i, sl(t0 - CTX, N))), in_=FMm((S_, i, n_), k=i))
                if i >= 4:
                    P.dma("sp", M.mF((b, sl((i - 4) * 128, 128), sl(t0, N))), FMm((S_, i, n_), k=i))
            if lat:
                for i in range(4):
                    z_ = zin[i]
                    P.op("act", "activation", out=M.zsT((S_, i, sl(t0 - CTX, N))), in_=z_((S_, n_)), func=AF.Silu)
            for c in range(N // 128):
                pt, st = ptr[itr % 2], tst[itr % 2]
                itr += 1
                for i in range(6):
                    P.op("pe", "transpose", out=pt((S_, i)), in_=FMm((S_, i, sl(c * 128, 128)), k=i), identity=G.ident_b())
                P.op("dve", "tensor_copy", out=st(), in_=V(pt(), lambda a: a.rearrange("p a b -> p (a b)")))
                P.dma("sp", M.mT((b, sl(t0 + c * 128, 128))), st())
        P.barrier()


def phaseB2b(G, b):
    P, I, nc, M, R = G.P, G.I, G.nc, G.M, G.R
    with ExitStack() as es:
        def sbl(name, shape, dt=F32, n=1):
            return Buf(es.enter_context(SBT(nc, name, list(shape), dt)), n)

        def psl(name, shape, dt=F32, n=1):
            return Buf(es.enter_context(PST(nc, name, list(shape), dt)), n)

        NB = 3
        mf = [[sbl("s_mf%d_%d" % (dd, i), [128, 4, 128], BF16) for i in range(NB)] for dd in range(2)]
        mt = [[sbl("s_mt%d_%d" % (dd, i), [128, 768], BF16) for i in range(NB)] for dd in range(2)]
        dtr = [[sbl("s_dtr%d_%d" % (dd, i), [128, 16]) for i in range(NB)] for dd in range(2)]
        sm = [[[sbl("s_%s%d_%d" % (n, dd, i), [128, 8]) for n in ("xx", "ee", "dt", "la", "acs", "nacs", "dec", "etot")]
               for i in range(2)] for dd in range(2)]
        X = [[sbl("s_X%d_%d" % (dd, i), [128, 512], BF16) for i in range(2)] for dd in range(2)]
        Xd = [[sbl("s_Xd%d_%d" % (dd, i), [128, 512], BF16) for i in range(2)] for dd in range(2)]
        cbm = [sbl("s_cbm%d" % i, [128, 128], BF16) for i in range(2)]
        LM = [sbl("s_LM%d" % i, [128, 4, 128]) for i in range(2)]
        Lm = [sbl("s_L%d" % i, [128, 4, 128]) for i in range(2)]
        EA = [sbl("s_EA%d" % i, [128, 4, 128]) for i in range(2)]
        Wm = [[[sbl("s_W%d_%d_%d" % (dd, i, g), [128, 4, 128], BF16) for g in range(2)] for i in range(2)] for dd in range(2)]
        Cp = [[[sbl("s_Cp%d_%d_%d" % (dd, i, g), [128, 4, 128], BF16) for g in range(2)] for i in range(2)] for dd in range(2)]
        S2 = [sbl("s_S%d" % dd, [128, 8, 64]) for dd in range(2)]
        Sb2 = [sbl("s_Sb%d" % dd, [128, 8, 64], BF16) for dd in range(2)]
        yv = V(M.yacc(), lambda a: a.rearrange("p a b -> p (a b)").rearrange("p (j t) -> p j t", j=4))
        pa = psl("s_pa", [128, 16])
        pcb = [psl("s_pcb%d" % i, [128, 128]) for i in range(2)]
        pR = [psl("s_pR%d" % i, [128, 4, 128]) for i in range(2)]
        pY = psl("s_pY", [128, 4, 128])
        pS = psl("s_pS", [128, 512])
        h3 = lambda a: a.rearrange("p (h n) -> p h n", n=64)
        bc64 = lambda a: a.unsqueeze(2).to_broadcast([128, 8, 64])
        f2 = lambda a: a.rearrange("p a b -> p (a b)")
        for cc in range(16):
            P.op("dve", "memset", ap=M.yacc((S_, cc)), constant=0.0)
        kk = {"g": 0}

        def indep(d, c, slot):
            mi = 0 if d == 0 else 2
            dc = sl(d * 8, 8)
            mf_, mt_, dtr_ = mf[d][slot % NB], mt[d][slot % NB], dtr[d][slot % NB]
            xx, ee, dt_, la, acs, nacs, dec, etot = sm[d][slot % 2]
            X_, Xd_ = X[d][slot % 2], Xd[d][slot % 2]
            tsl = sl(c * 128, 128)
            latent = c >= 2
            P.dma("sp", mf_(), V(M.mF((b, S_, tsl)), lambda a: a.rearrange("(q p) t -> p q t", p=128)))
            P.dma("sp", mt_(), M.mT((b, tsl)))
            P.dma("sp", dtr_(), G.dtt((b, tsl)))
            yield
            P.op("dve", "tensor_tensor", out=xx(), in0=dtr_((S_, dc)), in1=M.dtb((S_, dc)), op=ALU.add)
            P.op("act", "activation", out=ee(), in_=xx(), func=AF.Exp)
            P.op("act", "activation", out=dt_(), in_=ee(), func=AF.Ln, bias=M.onec())
            P.op("dve", "tensor_tensor", out=la(), in0=dt_(), in1=M.negA((S_, dc)), op=ALU.mult)
            P.op("pe", "matmul", out=pa((S_, sl(0, 8))), lhsT=R.masks_f((S_, mi)), rhs=la(), start=True, stop=True)
            P.op("pe", "matmul", out=pa((S_, sl(8, 8))), lhsT=M.ones_f(), rhs=la(), start=True, stop=True)
            P.op("dve", "tensor_copy", out=acs(), in_=pa((S_, sl(0, 8))))
            P.op("dve", "tensor_tensor", out=dec(), in0=pa((S_, sl(8, 8))), in1=acs(), op=ALU.subtract)
            P.op("act", "activation", out=dec(), in_=dec(), func=AF.Exp)
            P.op("act", "activation", out=etot(), in_=pa((S_, sl(8, 8))), func=AF.Exp)
            P.op("dve", "tensor_tensor", out=V(X_(), h3), in0=V(mt_((S_, sl(0, 512))), h3), in1=V(dt_(), bc64), op=ALU.mult)
            P.op("dve", "tensor_tensor", out=V(Xd_(), h3), in0=V(X_(), h3), in1=V(dec(), bc64), op=ALU.mult)
            yield
            if latent:
                for g in range(2):
                    hs = sl(4 * g, 4)
                    P.op("pe", "matmul", out=pcb[d](), lhsT=mf_((S_, g)), rhs=mf_((S_, 2 + g)), start=True, stop=True)
                    P.op("dve", "tensor_tensor", out=LM[d](),
                         in0=V(R.masks_f((S_, mi)), lambda a: a.unsqueeze(1).to_broadcast([128, 4, 128])),
                         in1=V(la((S_, hs)), lambda a: a.unsqueeze(2).to_broadcast([128, 4, 128])), op=ALU.mult)
                    P.op("pe", "matmul", out=V(pR[d](), f2), lhsT=M.ones_f(), rhs=V(LM[d](), f2), start=True, stop=True)
                    P.op("dve", "tensor_tensor", out=cbm[d](), in0=pcb[d](), in1=R.masks_b((S_, mi)), op=ALU.mult)
                    yield
                    for hh in range(4):
                        P.op("dve", "tensor_scalar", out=Lm[d]((S_, hh)), in0=pR[d]((S_, hh)),
                             scalar1=acs((S_, sl(4 * g + hh, 1))), scalar2=0.0, op0=ALU.subtract, op1=ALU.min)
                    P.op("act", "activation", out=Lm[d](), in_=Lm[d](), func=AF.Exp)
                    P.op("act", "activation", out=EA[d](), in_=pR[d](), func=AF.Exp)
                    yield
                    P.op("dve", "tensor_tensor", out=Wm[d][slot % 2][g](), in0=Lm[d](),
                         in1=V(cbm[d](), lambda a: a.unsqueeze(1).to_broadcast([128, 4, 128])), op=ALU.mult)
                    P.op("dve", "tensor_tensor", out=Cp[d][slot % 2][g](), in0=EA[d](),
                         in1=V(mf_((S_, 2 + g)), lambda a: a.unsqueeze(1).to_broadcast([128, 4, 128])), op=ALU.mult)
                    yield

        def dep(d, c, slot):
            mt_ = mt[d][slot % NB]
            etot = sm[d][slot % 2][7]
            X_, Xd_ = X[d][slot % 2], Xd[d][slot % 2]
            S, Sb = S2[d], Sb2[d]
            latent = c >= 2
            if latent:
                for g in range(2):
                    for hh in range(4):
                        h = 4 * g + hh
                        j, pb, hc = h // 2, sl((h % 2) * 64, 64), sl(h * 64, 64)
                        P.op("pe", "matmul", out=pY((pb, j)), lhsT=X_((S_, hc)), rhs=Wm[d][slot % 2][g]((S_, hh)), start=True, stop=False)
                        P.op("pe", "matmul", out=pY((pb, j)), lhsT=Sb((S_, h)), rhs=Cp[d][slot % 2][g]((S_, hh)), start=False, stop=True)
                yo = V(yv, lambda a: a[:, :, (c - 2) * 128:(c - 1) * 128])
                P.op("dve", "tensor_tensor", out=yo, in0=pY(), in1=yo, op=ALU.add)
            yield
            for g in range(2):
                P.op("pe", "matmul", out=pS((S_, sl(g * 256, 256))), lhsT=mt_((S_, sl(512 + g * 128, 128))),
                     rhs=Xd_((S_, sl(g * 256, 256))), start=True, stop=True)
            P.op("dve", "tensor_tensor", out=S(), in0=S(), in1=V(etot(), bc64), op=ALU.mult)
            P.op("dve", "tensor_tensor", out=V(S(), lambda a: a.rearrange("p h n -> p (h n)")), in0=pS(),
                 in1=V(S(), lambda a: a.rearrange("p h n -> p (h n)")), op=ALU.add)
            P.op("act", "activation", out=Sb(), in_=S(), func=AF.Copy)
            yield

        def stream(d):
            order = ORDER_F if d == 0 else ORDER_B
            for _ in indep(d, order[0], 0):
                yield
            P.op("dve", "memset", ap=S2[d](), constant=0.0)
            P.op("dve", "memset", ap=Sb2[d](), constant=0.0)
            for i, c in enumerate(order):
                gd = dep(d, c, i)
                gi = indep(d, order[i + 1], i + 1) if i + 1 < len(order) else iter(())
                i_done = d_done = False
                while not (i_done and d_done):
                    if not i_done:
                        i_done = next(gi, "end") == "end"
                    if not d_done:
                        d_done = next(gd, "end") == "end"
                    yield

        g0, g1 = stream(0), stream(1)
        a_done = b_done = False
        while not (a_done and b_done):
            if not a_done:
                a_done = next(g0, "end") == "end"
            if not b_done:
                b_done = next(g1, "end") == "end"
        P.barrier()


def phaseB2c(G, b):
    P, I, nc, M = G.P, G.I, G.nc, G.M
    with ExitStack() as es:
        def sbl(name, shape, dt=F32, n=1):
            return Buf(es.enter_context(SBT(nc, name, list(shape), dt)), n)

        def psl(name, shape, dt=F32, n=1):
            return Buf(es.enter_context(PST(nc, name, list(shape), dt)), n)

        yv = V(M.yacc(), lambda a: a.rearrange("p a b -> p (a b)").rearrange("p (j t) -> p j t", j=4))
        y = sbl("f_y", [128, 4, 512])
        sq = sbl("f_sq", [128, 4, 512], BF16)
        rs = sbl("f_rs", [128, 2, 512])
        yob = [sbl("f_yob%d" % i, [128, 4, 512], BF16) for i in range(2)]
        pq = [psl("f_pq%d" % i, [128, 512]) for i in range(2)]
        for q in range(4):
            tk = sl(q * 512, 512)
            yq = V(yv, lambda a: a[:, :, q * 512:(q + 1) * 512])
            P.op("dve", "tensor_tensor", out=y(), in0=M.xsT((S_, S_, tk)),
                 in1=V(M.dskip(), lambda a: a.unsqueeze(2).to_broadcast([128, 4, 512])), op=ALU.mult)
            P.op("dve", "tensor_tensor", out=y(), in0=y(), in1=yq, op=ALU.add)
            P.op("dve", "tensor_tensor", out=y(), in0=y(), in1=M.zsT((S_, S_, tk)), op=ALU.mult)
            P.op("act", "activation", out=sq(), in_=y(), func=AF.Square)
            for g in range(2):
                P.op("pe", "matmul", out=pq[g](), lhsT=M.ones_b(), rhs=sq((S_, 2 * g)), start=True, stop=False)
                P.op("pe", "matmul", out=pq[g](), lhsT=M.ones_b(), rhs=sq((S_, 2 * g + 1)), start=False, stop=True)
                P.op("act", "activation", out=rs((S_, g)), in_=pq[g](), func=AF.Sqrt, bias=G.epsc(), scale=1.0 / 256)
            P.op("dve", "reciprocal", out=rs(), in_=rs())
            for g in range(2):
                P.op("dve", "tensor_tensor", out=y((S_, sl(2 * g, 2))), in0=y((S_, sl(2 * g, 2))),
                     in1=V(rs((S_, g)), lambda a: a.unsqueeze(1).to_broadcast([128, 2, 512])), op=ALU.mult)
            yo_ = yob[q % 2]
            for j in range(4):
                P.op("act", "activation", out=yo_((S_, j)), in_=y((S_, j)), func=AF.Identity, scale=M.gnw((S_, sl(j, 1))))
            P.dma("sp", V(G.yTs((b, sl(512, 512), tk)), lambda a: a.rearrange("(j p) t -> p j t", p=128)), yo_())
        P.barrier()


def phaseB3(G, b):
    P, I, nc, M, R = G.P, G.I, G.nc, G.M, G.R
    with ExitStack() as es:
        def sbl(name, shape, dt=F32, n=1):
            return Buf(es.enter_context(SBT(nc, name, list(shape), dt)), n)

        def psl(name, shape, dt=F32, n=1):
            return Buf(es.enter_context(PST(nc, name, list(shape), dt)), n)

        xt = [sbl("o_xt%d" % i, [128, D]) for i in range(3)]
        x1 = [sbl("o_x1%d" % i, [128, D]) for i in range(2)]
        yt = [sbl("o_yt%d" % i, [128, 8, 128], BF16) for i in range(3)]
        gt = sbl("o_gt", [128, D])
        wout = sbl("o_wout", [128, 8, D], BF16)
        wv = V(I["w_out"](), lambda a: a.rearrange("(k p) n -> p k n", p=128))
        for k in range(8):
            P.dma("pool", wout((S_, k)), V(wv, lambda a: a[:, k]))
        P.dma("sp", gt(), G.gts((b, 0)))
        po = [psl("o_po%d" % i, [128, 512]) for i in range(4)]
        def issue(i):
            P.dma("sp", xt[i % 3](), I["xall"]((b, sl(CTX + i * 128, 128))))
            P.dma("sp", yt[i % 3](), V(G.yTs((b, S_, sl(i * 128, 128))), lambda a: a.rearrange("(k p) t -> p k t", p=128)))

        issue(0)
        issue(1)
        for i in range(16):
            x_, x1_, yt_ = xt[i % 3], x1[i % 2], yt[i % 3]
            tk = sl(i * 128, 128)
            if i + 2 < 16:
                issue(i + 2)
            for hf in range(2):
                p_ = po[(2 * i + hf) % 4]
                for k in range(8):
                    P.op("pe", "matmul", out=p_(), lhsT=yt_((S_, k)), rhs=wout((S_, k, sl(hf * 512, 512))), start=(k == 0), stop=(k == 7))
                cs = sl(hf * 512, 512)
                P.op("dve", "tensor_tensor", out=x1_((S_, cs)), in0=p_(), in1=gt((S_, cs)), op=ALU.mult)
                P.op("dve", "tensor_tensor", out=x1_((S_, cs)), in0=x1_((S_, cs)), in1=x_((S_, cs)), op=ALU.add)
            P.dma("sp", G.x1s((b, tk)), x1_())
        P.barrier()


def phaseC(G):
    P, I, nc, nseq = G.P, G.I, G.nc, G.nseq
    with ExitStack() as es:
        def sbl(name, shape, dt=F32, n=1):
            return Buf(es.enter_context(SBT(nc, name, list(shape), dt)), n)

        def psl(name, shape, dt=F32, n=1):
            return Buf(es.enter_context(PST(nc, name, list(shape), dt)), n)

        NT = 512
        wgu = sbl("c_wgu", [128, 8, 2 * D_FF], BF16)
        wdn = sbl("c_wdn", [128, 22, D], BF16)
        fnw = sbl("c_fnw", [128, D])
        gv = V(I["w_gu"](), lambda a: a.rearrange("(k p) n -> p k n", p=128))
        for k in range(8):
            for c0 in range(0, 2 * D_FF, 1408):
                P.dma("pool", wgu((S_, k, sl(c0, 1408))), V(gv, lambda a: a[:, k, c0:c0 + 1408]))
        dv = V(I["w_down"](), lambda a: a.rearrange("(k p) n -> p k n", p=128))
        for k in range(22):
            P.dma("pool", wdn((S_, k)), V(dv, lambda a: a[:, k]))
        P.dma("sp", fnw(), V(I["fnw"](), lambda a: a.partition_broadcast(128)))
        xt = [sbl("c_xt%d" % i, [128, D]) for i in range(2)]
        xr = [sbl("c_xr%d" % i, [128, D]) for i in range(2)]
        gt = sbl("c_gt", [128, D])
        xn = [sbl("c_xn%d" % i, [128, D], BF16) for i in range(2)]
        junk = sbl("c_junk", [128, D], BF16)
        ss = [sbl("c_ss%d" % i, [128, 1]) for i in range(4)]
        rstd = [sbl("c_rstd%d" % i, [128, 1]) for i in range(4)]
        xm = [sbl("c_xm%d" % i, [128, 8, NT], BF16) for i in range(2)]
        act = sbl("c_act", [128, 22, NT], BF16)
        sg = [sbl("c_sg%d" % i, [128, NT], BF16) for i in range(2)]
        x2 = [sbl("c_x2%d" % i, [128, 512]) for i in range(1)] * 2
        tp = [psl("c_tp%d" % i, [128, 8, 128], BF16) for i in range(2)]
        pg = [psl("c_pg%d" % i, [128, NT]) for i in range(2)]
        pu = [psl("c_pu%d" % i, [128, NT]) for i in range(2)]
        pd = [psl("c_pd%d" % i, [128, 512]) for i in range(2)]
        st_ = {"it": 0, "ij": 0, "ipd": 0, "ie": 0}
        blocks = [(b, blk) for b in range(nseq) for blk in range(SEQ // NT)]

        def front(k):
            b, blk = blocks[k]
            xm_ = xm[k % 2]
            for il in range(NT // 128):
                tok0 = blk * NT + il * 128
                it = st_["it"]
                st_["it"] += 1
                x_ = xt[it % 2]
                xn_, ss_, rs_, tp_ = xn[it % 2], ss[it % 2], rstd[it % 2], tp[it % 2]
                P.dma("sp", x_(), G.x1s((b, sl(tok0, 128))))
                P.op("act", "activation", out=junk(), in_=x_(), func=AF.Square, accum_out=ss_())
                P.op("act", "activation", out=rs_(), in_=ss_(), func=AF.Sqrt, bias=G.epsc(), scale=1.0 / D)
                P.op("dve", "reciprocal", out=rs_(), in_=rs_())
                P.op("dve", "tensor_scalar", out=xn_(), in0=x_(), scalar1=rs_(), scalar2=None, op0=ALU.mult)
                yield
                for k8 in range(8):
                    P.op("pe", "transpose", out=tp_((S_, k8)), in_=xn_((S_, sl(k8 * 128, 128))), identity=G.ident_b())
                for k8 in range(8):
                    o = xm_((S_, k8, sl(il * 128, 128)))
                    s1 = G.scale2((S_, k8, sl(b, 1)))
                    s2 = G.modT((S_, 24 + k8, sl(b, 1)))
                    if k8 % 2 == 0:
                        P.op("dve", "tensor_scalar", out=o, in0=tp_((S_, k8)), scalar1=s1, scalar2=s2, op0=ALU.mult, op1=ALU.add)
                    else:
                        P.op("act", "activation", out=o, in_=tp_((S_, k8)), func=AF.Identity, bias=s2, scale=s1)
                yield
                yield

        def body(k):
            b, blk = blocks[k]
            xm_ = xm[k % 2]
            if blk == 0:
                P.dma("sp", gt(), G.gts((b, 1)))
            for j in range(22):
                ij = st_["ij"]
                st_["ij"] += 1
                pg_, pu_, sg_ = pg[ij % 2], pu[ij % 2], sg[ij % 2]
                for k8 in range(8):
                    P.op("pe", "matmul", out=pg_(), lhsT=wgu((S_, k8, sl(j * 128, 128))), rhs=xm_((S_, k8)), start=(k8 == 0), stop=(k8 == 7))
                for k8 in range(8):
                    P.op("pe", "matmul", out=pu_(), lhsT=wgu((S_, k8, sl(D_FF + j * 128, 128))), rhs=xm_((S_, k8)),
                         start=(k8 == 0), stop=(k8 == 7))
                P.op("act", "activation", out=sg_(), in_=pg_(), func=AF.Silu)
                P.op("dve", "tensor_tensor", out=act((S_, j)), in0=pu_(), in1=sg_(), op=ALU.mult)
                yield
            for il in range(NT // 128):
                tok0 = blk * NT + il * 128
                ie = st_["ie"]
                st_["ie"] += 1
                x_ = xr[ie % 2]
                ss_, rs_ = ss[2 + ie % 2], rstd[2 + ie % 2]
                P.dma("sp", x_(), G.x1s((b, sl(tok0, 128))))
                for hf in range(2):
                    p_ = pd[st_["ipd"] % 2]
                    st_["ipd"] += 1
                    cs = sl(hf * 512, 512)
                    for j in range(22):
                        P.op("pe", "matmul", out=p_(), lhsT=act((S_, j, sl(il * 128, 128))), rhs=wdn((S_, j, cs)),
                             start=(j == 0), stop=(j == 21))
                    x2_ = x2[hf]
                    P.op("dve", "tensor_tensor", out=x2_(), in0=p_(), in1=gt((S_, cs)), op=ALU.mult)
                    P.op("dve", "tensor_tensor", out=x_((S_, cs)), in0=x2_(), in1=x_((S_, cs)), op=ALU.add)
                    yield
                P.op("act", "activation", out=junk(), in_=x_(), func=AF.Square, accum_out=ss_())
                P.op("act", "activation", out=rs_(), in_=ss_(), func=AF.Sqrt, bias=G.epsc(), scale=1.0 / D)
                P.op("dve", "reciprocal", out=rs_(), in_=rs_())
                P.op("dve", "scalar_tensor_tensor", out=x_(), in0=x_(), scalar=rs_(), in1=fnw(), op0=ALU.mult, op1=ALU.mult)
                P.dma("sp", G.out((b, sl(tok0, 128))), x_())
                yield

        def rr(ga, gb, ratio=2):
            a_done = b_done = False
            while not (a_done and b_done):
                for _ in range(ratio):
                    if not a_done:
                        a_done = next(ga, "end") == "end"
                if not b_done:
                    b_done = next(gb, "end") == "end"

        for _ in front(0):
            pass
        for k in range(len(blocks)):
            rr(body(k), front(k + 1) if k + 1 < len(blocks) else iter(()), ratio=2)
        P.barrier()


def kernel(**inputs):
    inputs = {k: np.asarray(v) for k, v in inputs.items()}
    nc, G = build()
    in_maps = [host_inputs(inputs, i) for i in range(NCORES)]
    res = run_bass_kernel_spmd(nc, in_maps, core_ids=list(range(NCORES)))
    out = np.concatenate([np.asarray(r["out"]).reshape(NSEQ, SEQ, D) for r in res.results], axis=0)
    return np.ascontiguousarray(out, dtype=np.float32)


def seq_stack_b1(G):
    G.P.barrier()
    G.es_seq = ExitStack()
    nc = G.nc
    G.R.bonusT = Buf(G.es_seq.enter_context(SBT(nc, "rw_bonusT", [128, 4, SEQ], BF16)))
    G.R.gT = Buf(G.es_seq.enter_context(SBT(nc, "rw_gT", [128, 4, SEQ], BF16)))


def seq_stack_b2(G):
    G.P.barrier()
    G.es_seq = ExitStack()
    nc = G.nc
    G.M.xsT = Buf(G.es_seq.enter_context(SBT(nc, "mb_xsT", [128, 4, SEQ], BF16)))
    G.M.zsT = Buf(G.es_seq.enter_context(SBT(nc, "mb_zsT", [128, 4, SEQ], BF16)))
    if not hasattr(G.M, "cdiag_done"):
        pass
    G.M.cdiag = Buf(G.es_seq.enter_context(SBT(nc, "mb_cdiag", [128, 8, 9, 128], BF16)))
    for tl in range(8):
        for tap in range(9):
            G.P.op("dve", "tensor_scalar", out=G.M.cdiag((S_, tl, tap)), in0=G.ident_f(), scalar1=G.M.conv((S_, tl, sl(tap, 1))),
                   scalar2=None, op0=ALU.mult)
```

```python
import numpy as np
import ml_dtypes
from contextlib import ExitStack
import concourse.bass as bass
import concourse.mybir as mybir
from concourse.bass_utils import run_bass_kernel_spmd

F32 = mybir.dt.float32
BF16 = mybir.dt.bfloat16
AF = mybir.ActivationFunctionType
ALU = mybir.AluOpType
AX = mybir.AxisListType

NCORES = 8
NSEQ = 4
D = 1024
SEQ = 2048
CTX = 256
T = CTX + SEQ
NCH = T // 128
IN_COLS = 3472
D_FF = 2816
EPS = 1e-6
RW_LN_EPS = 64e-5

ENGS = ("pe", "dve", "act", "pool", "sp")
NO_SELF_SYNC = ("pe",)
SEM_LIMIT = 24000
NDMASEM = 24


_UID = [0]


def SBT(nc, name, shape, dt):
    _UID[0] += 1
    return nc.sbuf_tensor("%s_%d" % (name, _UID[0]), shape, dt)


def PST(nc, name, shape, dt):
    _UID[0] += 1
    return nc.psum_tensor("%s_%d" % (name, _UID[0]), shape, dt)


class Trk:
    __slots__ = ("w", "r")

    def __init__(self):
        self.w = {}
        self.r = {}


class View:
    __slots__ = ("ap", "trks")

    def __init__(self, ap, trks):
        self.ap = ap
        self.trks = trks


class Buf:
    def __init__(self, h, n=1):
        self.h = h
        self.t = [Trk() for _ in range(n)]

    def __call__(self, idx=None, k=None):
        ap = self.h[:] if idx is None else self.h[idx]
        if k is None:
            trks = self.t
        elif isinstance(k, int):
            trks = [self.t[k]]
        else:
            trks = [self.t[i] for i in k]
        return View(ap, trks)


def V(view, f):
    return View(f(view.ap), view.trks)


class Prog:
    def __init__(self, nc):
        self.nc = nc
        self.q = {e: [] for e in ENGS}
        self.cnt = {}
        self.seen = {e: {} for e in ENGS}
        self.dma_rr = {e: 0 for e in ENGS}
        self.nops = 0

    def _emit(self, eng, fn, reads, writes, dma=False):
        deps = {}

        def need(kv):
            if kv is None:
                return
            k, v = kv
            if deps.get(k, 0) < v:
                deps[k] = v

        for t in reads:
            for kv in t.w.items():
                need(kv)
        for t in writes:
            for kv in t.w.items():
                need(kv)
            for kv in t.r.items():
                need(kv)
        if dma:
            slot = self.dma_rr[eng]
            self.dma_rr[eng] = (slot + 1) % NDMASEM
            base = ("dma", eng, slot)
            ep = self.cnt.get(base, 0) // 2000
            key = ("dma", eng, slot, ep)
            prev = self.cnt.get(base, 0)
            if prev > 0:
                pk = ("dma", eng, slot, (prev - 1) // 2000)
                need((pk, prev - ((prev - 1) // 2000) * 2000))
            self.cnt[base] = prev + 1
            val = prev + 1 - ep * 2000
        else:
            base = ("eng", eng)
            prev = self.cnt.get(base, 0)
            ep = prev // SEM_LIMIT
            key = ("eng", eng, ep)
            self.cnt[base] = prev + 1
            val = prev + 1 - ep * SEM_LIMIT
        seen = self.seen[eng]
        for k, v in deps.items():
            if k[0] == "eng" and k[1] == eng and eng in NO_SELF_SYNC:
                continue
            if seen.get(k, 0) >= v:
                continue
            seen[k] = v
            self.q[eng].append(("wait", k, v))
        self.q[eng].append(("op", fn, key))
        kv = (key, val)
        for t in reads:
            if t.r.get(key, 0) < val:
                t.r[key] = val
        for t in writes:
            if dma or any(k2[0] == "dma" for k2 in t.w):
                t.w[key] = max(t.w.get(key, 0), val)
            else:
                t.w = {key: val}
            t.r = {}
        self.nops += 1

    def op(self, eng, method, **kw):
        reads, writes = [], []
        apkw = {}
        for name, a in kw.items():
            if isinstance(a, View):
                apkw[name] = a.ap
                if name in ("out", "ap", "accum_out"):
                    writes.extend(a.trks)
                else:
                    reads.extend(a.trks)
            else:
                apkw[name] = a
        acc = kw.get("_acc")
        if method == "matmul" and kw.get("start") is False:
            pass
        fn = lambda e, m=method, k=apkw: getattr(e, m)(**k)
        self._emit(eng, fn, reads, writes)

    def dma(self, eng, out, in_, **kw):
        fn = lambda e, o=out.ap, i=in_.ap, k=kw: e.dma_start(out=o, in_=i, **k)
        self._emit(eng, fn, list(in_.trks), list(out.trks), dma=True)

    def barrier(self):
        allk = []
        for base, c in self.cnt.items():
            if c == 0:
                continue
            if base[0] == "eng":
                ep = (c - 1) // SEM_LIMIT
                allk.append((("eng", base[1], ep), c - ep * SEM_LIMIT))
            else:
                ep = (c - 1) // 2000
                allk.append((("dma", base[1], base[2], ep), c - ep * 2000))
        for eng in ENGS:
            seen = self.seen[eng]
            for k, v in allk:
                if seen.get(k, 0) >= v:
                    continue
                seen[k] = v
                self.q[eng].append(("wait", k, v))

    def finish(self):
        nc = self.nc
        self.barrier()
        sems = {}

        def sem(k):
            if k not in sems:
                sems[k] = nc.alloc_semaphore("s_" + "_".join(str(x) for x in k))
            return sems[k]

        for eng in ENGS:
            for it in self.q[eng]:
                sem(it[2] if it[0] == "op" else it[1])

        def run(e, eng):
            for it in self.q[eng]:
                if it[0] == "wait":
                    k, v = it[1], it[2]
                    e.wait_ge(sem(k), v * 16 if k[0] == "dma" else v)
                else:
                    fn, key = it[1], it[2]
                    fn(e).then_inc(sem(key), 16 if key[0] == "dma" else 1)

        with nc.Block() as block:
            @block.tensor
            def _(e):
                run(e, "pe")

            @block.vector
            def _(e):
                run(e, "dve")

            @block.scalar
            def _(e):
                run(e, "act")

            @block.gpsimd
            def _(e):
                run(e, "pool")

            @block.sync
            def _(e):
                run(e, "sp")
        return len(sems)


class Ctx:
    pass


def build(nseq=NSEQ, debug=None, stop=None):
    nc = bass.Bass("TRN2", target_bir_lowering=False)
    P = Prog(nc)
    es = ExitStack()
    G = Ctx()
    G.nc, G.P, G.nseq, G.debug = nc, P, nseq, debug

    def din(name, shape, dt=F32):
        return Buf(nc.dram_tensor(name, list(shape), dt, kind="ExternalInput"))

    def dscr(name, shape, dt=F32, kind="Internal"):
        return Buf(nc.dram_tensor(name, list(shape), dt, kind=kind))

    G.din, G.dscr = din, dscr
    I = {}
    I["xall"] = din("xall", [nseq, T, D])
    I["cT"] = din("cT", [128, 8, 5])
    I["mod_w"] = din("mod_w", [D, 6 * D])
    I["mod_bT"] = din("mod_bT", [128, 48])
    I["mod_b"] = din("mod_b", [1, 6 * D])
    I["norm1T"] = din("norm1T", [128, 8])
    I["norm2T"] = din("norm2T", [128, 8])
    I["w_in"] = din("w_in", [D, IN_COLS])
    I["ident"] = din("ident", [128, 128])
    for nm, shp in (("tshiftT", [128, 15, 3]), ("w0T", [128, 4, 2]), ("a0T", [128, 4, 2]), ("kkT", [128, 4]),
                    ("kaT", [128, 4]), ("rkT", [128, 4]), ("lnwT", [128, 4]), ("lnbT", [128, 4]),
                    ("w2s", [128, 512]), ("a2s", [128, 512]), ("g2", [128, 512]), ("bones", [128, 128]),
                    ("masks", [128, 4, 128]), ("rmask", [128, 512]), ("convT", [128, 8, 9]), ("convbT", [128, 8]),
                    ("dtb", [128, 16]), ("alog", [128, 16]), ("dskipT", [128, 4]), ("gnwT", [128, 4]),
                    ("w_out", [D, D]), ("w_gu", [D, 2 * D_FF]), ("w_down", [D_FF, D]), ("fnw", [1, D]), ("cmask", [128, 2, 642])):
        I[nm] = din(nm, shp)
    G.I = I

    def sb(name, shape, dt=F32, n=1):
        return Buf(es.enter_context(SBT(nc, name, list(shape), dt)), n)

    def ps(name, shape, dt=F32, n=1):
        return Buf(es.enter_context(PST(nc, name, list(shape), dt)), n)

    G.sb, G.ps = sb, ps

    G.ident_f = sb("ident_f", [128, 128])
    G.ident_b = sb("ident_b", [128, 128], BF16)
    G.modT = sb("modT", [128, 48, 5])
    G.scale1 = sb("scale1", [128, 8, 5])
    G.scale2 = sb("scale2", [128, 8, 5])
    G.gts = dscr("gts", [nseq, 2, 128, D], F32)
    G.epsc = sb("epsc", [128, 1])
    P.op("dve", "memset", ap=G.epsc(), constant=EPS)
    P.dma("sp", G.ident_f(), I["ident"]())
    P.op("dve", "tensor_copy", out=G.ident_b(), in_=G.ident_f())

    G.uT = dscr("uT", [nseq, 27 * 128, T], BF16)
    G.uTf = dscr("uTf", [nseq, 3 * 128, T], F32)
    G.dtt = dscr("dtt", [nseq, T, 16], F32)

    es_A = ExitStack()
    G.w_in_bf = Buf(es_A.enter_context(SBT(nc, "w_in_bf", [128, 8, IN_COLS], BF16)))
    wv = V(I["w_in"](), lambda a: a.rearrange("(k p) n -> p k n", p=128))
    for k in range(8):
        for c0 in range(0, IN_COLS, 1736):
            P.dma("pool", G.w_in_bf((slice(None), k, slice(c0, c0 + 1736))), V(wv, lambda a: a[:, k, c0:c0 + 1736]))
    phase0(G)
    phaseA(G)
    es_A.close()

    if debug == "A":
        o1 = dscr("o_uT", [27 * 128, T], BF16, kind="ExternalOutput")
        o2 = dscr("o_uTf", [3 * 128, T], F32, kind="ExternalOutput")
        o3 = dscr("o_dtt", [T, 16], F32, kind="ExternalOutput")
        o4 = dscr("o_modT", [128, 48 * 5], F32, kind="ExternalOutput")
        o5 = dscr("o_gtbc", [128, 2 * D], F32, kind="ExternalOutput")
        P.barrier()
        P.dma("sp", o1(), G.uT((0,)))
        P.dma("sp", o2(), G.uTf((0,)))
        P.dma("sp", o3(), G.dtt((0,)))
        P.dma("sp", o4(), V(G.modT(), lambda a: a.rearrange("p a b -> p (a b)")))
        P.dma("sp", V(o5(), lambda a: a.rearrange("p (a b) -> a p b", a=2)), G.gts((0,)))

    G.x1s = dscr("x1s", [nseq, SEQ, D], F32)
    if debug is None:
        G.out = dscr("out", [nseq, SEQ, D], F32, kind="ExternalOutput")
        es_mix = ExitStack()
        rw_setup(G, es_mix)
        mb_setup(G, es_mix)
        P.barrier()
        for b in range(nseq):
            if stop == "A":
                break
            seq_stack_b1(G)
            phaseB1a(G, b)
            if stop != "B1a":
                phaseB1b(G, b)
                phaseB1c(G, b)
            G.es_seq.close()
            if stop in ("B1", "B1a"):
                continue
            seq_stack_b2(G)
            phaseB2a(G, b)
            if stop != "B2a":
                phaseB2b(G, b)
                phaseB2c(G, b)
            G.es_seq.close()
            if stop in ("B2", "B2a"):
                continue
            phaseB3(G, b)
        es_mix.close()
        if stop is None:
            phaseC(G)
    nsem = P.finish()
    es.close()
    G.nsem = nsem
    return nc, G


def phase0(G):
    P, I, sb, ps, nseq = G.P, G.I, G.sb, G.ps, G.nseq
    with ExitStack() as es:
        def sbl(name, shape, dt=F32, n=1):
            return Buf(es.enter_context(SBT(G.nc, name, list(shape), dt)), n)

        def psl(name, shape, dt=F32, n=1):
            return Buf(es.enter_context(PST(G.nc, name, list(shape), dt)), n)

        cT = sbl("cT_sb", [128, 8, 5])
        sc = sbl("sc_sb", [128, 8, 5])
        screp = sbl("screp", [128, nseq, 8, 128])
        mbT = sbl("mbT", [128, 48])
        mbrow = sbl("mbrow", [128, 2, D])
        n1 = sbl("n1", [128, 8])
        n2 = sbl("n2", [128, 8])
        mw = [sbl("mw%d" % i, [128, 8, D]) for i in range(2)]
        pm = [psl("pm%d" % i, [128, 8, 5]) for i in range(2)]
        pg = [psl("pg%d" % i, [128, 512]) for i in range(2)]
        gst = [sbl("gst%d" % i, [128, 512]) for i in range(2)]
        P.dma("sp", cT(), I["cT"]())
        P.dma("sp", mbT(), I["mod_bT"]())
        P.dma("sp", n1(), I["norm1T"]())
        P.dma("sp", n2(), I["norm2T"]())
        for qi, q in enumerate((2, 5)):
            P.dma("sp", mbrow((slice(None), qi)),
                  V(I["mod_b"]((slice(None), slice(q * D, (q + 1) * D))), lambda a: a.partition_broadcast(128)))
        P.op("act", "activation", out=sc(), in_=cT(), func=AF.Silu)
        for b in range(nseq):
            P.op("dve", "tensor_copy", out=screp((slice(None), b)),
                 in_=V(sc((slice(None), slice(None), slice(b, b + 1))), lambda a: a.to_broadcast([128, 8, 128])))
        mwv = V(I["mod_w"](), lambda a: a.rearrange("(k p) n -> p k n", p=128))
        ipg = 0
        for q in range(6):
            m = mw[q % 2]
            P.dma("sp", m(), V(mwv, lambda a: a[:, :, q * D:(q + 1) * D]))
            pmq = pm[q % 2]
            for j in range(8):
                for k in range(8):
                    P.op("pe", "matmul", out=pmq((slice(None), j)), lhsT=m((slice(None), k, slice(j * 128, (j + 1) * 128))),
                         rhs=sc((slice(None), k)), start=(k == 0), stop=(k == 7))
            for j in range(8):
                P.op("dve", "tensor_scalar", out=G.modT((slice(None), 8 * q + j)), in0=pmq((slice(None), j)),
                     scalar1=mbT((slice(None), slice(8 * q + j, 8 * q + j + 1))), scalar2=None, op0=ALU.add)
            if q in (2, 5):
                qi = 0 if q == 2 else 1
                for b in range(nseq):
                    for hf in range(2):
                        pgq = pg[ipg % 2]
                        ipg += 1
                        for k in range(8):
                            P.op("pe", "matmul", out=pgq(), lhsT=screp((slice(None), b, k)),
                                 rhs=m((slice(None), k, slice(hf * 512, (hf + 1) * 512))), start=(k == 0), stop=(k == 7))
                        gs_ = gst[ipg % 2]
                        P.op("dve", "tensor_tensor", out=gs_(),
                             in0=pgq(), in1=mbrow((slice(None), qi, slice(hf * 512, (hf + 1) * 512))), op=ALU.add)
                        P.dma("sp", G.gts((b, qi, slice(None), slice(hf * 512, (hf + 1) * 512))), gs_())
        for (dst, nw, q) in ((G.scale1, n1, 1), (G.scale2, n2, 4)):
            P.op("dve", "tensor_scalar", out=dst(), in0=G.modT((slice(None), slice(8 * q, 8 * q + 8))),
                 scalar1=1.0, scalar2=None, op0=ALU.add)
            P.op("dve", "tensor_tensor", out=dst(), in0=dst(),
                 in1=V(nw(), lambda a: a.unsqueeze(2).to_broadcast([128, 8, 5])), op=ALU.mult)
        P.barrier()


def phaseA(G):
    P, I, nseq, nc = G.P, G.I, G.nseq, G.nc
    with ExitStack() as es:
        def sbl(name, shape, dt=F32, n=1):
            return Buf(es.enter_context(SBT(nc, name, list(shape), dt)), n)

        def psl(name, shape, dt=F32, n=1):
            return Buf(es.enter_context(PST(nc, name, list(shape), dt)), n)

        w = G.w_in_bf
        NXT = 3
        xt = [sbl("xt%d" % i, [128, D]) for i in range(NXT)]
        xn = [sbl("xn%d" % i, [128, D], BF16) for i in range(2)]
        junk = sbl("junkA", [128, D], BF16)
        ss = [sbl("ss%d" % i, [128, 1]) for i in range(2)]
        rstd = [sbl("rstd%d" % i, [128, 1]) for i in range(2)]
        xm = [sbl("xm%d" % i, [128, 8, 512], BF16) for i in range(2)]
        stg_b = [sbl("stgb%d" % i, [128, 512], BF16) for i in range(4)]
        stg_f = [sbl("stgf%d" % i, [128, 512], F32) for i in range(2)]
        stg_d = [sbl("stgd%d" % i, [128, 4, 16], F32) for i in range(2)]
        tp = [psl("tpA%d" % i, [128, 8, 128], BF16) for i in range(2)]
        pmm = [psl("pmmA%d" % i, [128, 512]) for i in range(4)]
        pdt = [psl("pdtA%d" % i, [128, 4, 16]) for i in range(2)]
        st_ = {"it": 0, "imm": 0, "isb": 0, "isf": 0}
        groups = [(b, t0_, n_) for b in range(nseq) for (t0_, n_) in ([(0, 2)] + [(2 + 4 * g, 4) for g in range(4)])]

        def front(gi):
            b, tile0, ntile = groups[gi]
            bsel = 4 if tile0 == 0 else b
            xmg = xm[gi % 2]
            for il in range(ntile):
                t0 = (tile0 + il) * 128
                it = st_["it"]
                st_["it"] += 1
                x_, xn_, ss_, rs_, tp_ = xt[it % NXT], xn[it % 2], ss[it % 2], rstd[it % 2], tp[it % 2]
                P.dma("sp", x_(), I["xall"]((b, slice(t0, t0 + 128))))
                P.op("act", "activation", out=junk(), in_=x_(), func=AF.Square, accum_out=ss_())
                P.op("act", "activation", out=rs_(), in_=ss_(), func=AF.Sqrt, bias=G.epsc(), scale=1.0 / D)
                P.op("dve", "reciprocal", out=rs_(), in_=rs_())
                P.op("dve", "tensor_scalar", out=xn_(), in0=x_(), scalar1=rs_(), scalar2=None, op0=ALU.mult)
                yield
                for k in range(8):
                    P.op("pe", "transpose", out=tp_((slice(None), k)), in_=xn_((slice(None), slice(k * 128, (k + 1) * 128))),
                         identity=G.ident_b())
                for k in range(8):
                    o = xmg((slice(None), k, slice(il * 128, (il + 1) * 128)))
                    s1 = G.scale1((slice(None), k, slice(bsel, bsel + 1)))
                    s2 = G.modT((slice(None), k, slice(bsel, bsel + 1)))
                    if k % 2 == 0:
                        P.op("dve", "tensor_scalar", out=o, in0=tp_((slice(None), k)), scalar1=s1, scalar2=s2,
                             op0=ALU.mult, op1=ALU.add)
                    else:
                        P.op("act", "activation", out=o, in_=tp_((slice(None), k)), func=AF.Identity, bias=s2, scale=s1)
                yield
                yield

        def body(gi):
            b, tile0, ntile = groups[gi]
            xmg = xm[gi % 2]
            N = ntile * 128
            tsl = slice(tile0 * 128, tile0 * 128 + N)
            for j in range(27):
                pm_ = pmm[st_["imm"] % 4]
                st_["imm"] += 1
                for k in range(8):
                    P.op("pe", "matmul", out=pm_((slice(None), slice(0, N))), lhsT=w((slice(None), k, slice(j * 128, (j + 1) * 128))),
                         rhs=xmg((slice(None), k, slice(0, N))), start=(k == 0), stop=(k == 7))
                if 12 <= j <= 14:
                    st = stg_f[st_["isf"] % 2]
                    st_["isf"] += 1
                    P.op("dve", "tensor_copy", out=st((slice(None), slice(0, N))), in_=pm_((slice(None), slice(0, N))))
                    P.dma("sp", G.uTf((b, slice((j - 12) * 128, (j - 11) * 128), tsl)), st((slice(None), slice(0, N))))
                else:
                    st = stg_b[st_["isb"] % 4]
                    if st_["isb"] % 2 == 0:
                        P.op("act", "activation", out=st((slice(None), slice(0, N))), in_=pm_((slice(None), slice(0, N))),
                             func=AF.Copy)
                    else:
                        P.op("dve", "tensor_copy", out=st((slice(None), slice(0, N))), in_=pm_((slice(None), slice(0, N))))
                    st_["isb"] += 1
                    P.dma("sp", G.uT((b, slice(j * 128, (j + 1) * 128), tsl)), st((slice(None), slice(0, N))))
                yield
            pd_ = pdt[gi % 2]
            sd_ = stg_d[gi % 2]
            for il in range(ntile):
                for k in range(8):
                    P.op("pe", "matmul", out=pd_((slice(None), il)), lhsT=xmg((slice(None), k, slice(il * 128, (il + 1) * 128))),
                         rhs=w((slice(None), k, slice(3456, 3472))), start=(k == 0), stop=(k == 7))
            P.op("dve", "tensor_copy", out=sd_((slice(None), slice(0, ntile))), in_=pd_((slice(None), slice(0, ntile))))
            P.dma("sp", V(G.dtt((b, tsl)), lambda a: a.rearrange("(i p) c -> p i c", p=128)), sd_((slice(None), slice(0, ntile))))
            yield

        def rr(ga, gb, ratio=2):
            a_done = b_done = False
            while not (a_done and b_done):
                for _ in range(ratio):
                    if not a_done:
                        a_done = next(ga, "end") == "end"
                if not b_done:
                    b_done = next(gb, "end") == "end"

        for _ in front(0):
            pass
        for gi in range(len(groups)):
            rr(body(gi), front(gi + 1) if gi + 1 < len(groups) else iter(()), ratio=2)
        P.barrier()


def host_inputs(inputs, core, nseq=NSEQ):
    b0 = core * NSEQ
    x = inputs["x"][b0:b0 + nseq]
    ctx = inputs["ctx"][b0:b0 + nseq]
    m = {}
    m["xall"] = np.ascontiguousarray(np.concatenate([ctx, x], axis=1))
    cc = np.concatenate([inputs["c"][b0:b0 + NSEQ], inputs["c_ctx"][None]], axis=0)
    m["cT"] = np.ascontiguousarray(cc.reshape(5, 8, 128).transpose(2, 1, 0))
    m["mod_w"] = np.ascontiguousarray(inputs["mod_w"][0])
    m["mod_bT"] = np.ascontiguousarray(inputs["mod_b"][0].reshape(48, 128).T)
    m["mod_b"] = np.ascontiguousarray(inputs["mod_b"][0][None])
    m["norm1T"] = np.ascontiguousarray(inputs["norm1_w"][0].reshape(8, 128).T)
    m["norm2T"] = np.ascontiguousarray(inputs["norm2_w"][0].reshape(8, 128).T)
    m["w_in"] = np.ascontiguousarray(inputs["w_in"][0])
    m["ident"] = np.eye(128, dtype=np.float32)
    f32 = lambda a: np.ascontiguousarray(a, dtype=np.float32)
    colT = lambda v, n: f32(np.asarray(v).reshape(n, 128).T)
    m["tshiftT"] = f32(inputs["tshift_w"][0].reshape(3, 15, 128).transpose(2, 1, 0))
    m["w0T"] = f32(inputs["w0"][0].reshape(2, 4, 128).transpose(2, 1, 0))
    m["a0T"] = f32(inputs["a0"][0].reshape(2, 4, 128).transpose(2, 1, 0))
    m["kkT"] = colT(inputs["k_k"][0], 4)
    m["kaT"] = colT(inputs["k_a"][0], 4)
    m["rkT"] = colT(inputs["r_k"][0].reshape(-1), 4)
    m["lnwT"] = colT(inputs["lnx_w"][0], 4)
    m["lnbT"] = colT(inputs["lnx_b"][0], 4)
    m["w2s"] = f32(inputs["w2"][0].reshape(128, 512))
    m["a2s"] = f32(inputs["a2"][0].reshape(128, 512))
    m["g2"] = f32(inputs["g2"][0])
    bo = np.zeros((128, 128), np.float32); bo[:64, :64] = 1; bo[64:, 64:] = 1
    m["bones"] = bo
    p = np.arange(128)[:, None]; f = np.arange(128)[None, :]
    m["masks"] = f32(np.stack([p <= f, p < f, p >= f, p > f], axis=1))
    rm = np.ones((128, 512), np.float32); rm[:, ::128] = 0
    m["rmask"] = rm
    m["convT"] = f32(inputs["conv_w"][0].reshape(9, 8, 128).transpose(2, 1, 0))
    m["convbT"] = colT(inputs["conv_b"][0], 8)
    m["dtb"] = f32(np.broadcast_to(inputs["dt_bias"][0].reshape(1, 16), (128, 16)))
    m["alog"] = f32(np.broadcast_to(inputs["a_log"][0].reshape(1, 16), (128, 16)))
    m["dskipT"] = colT(np.repeat(inputs["d_skip"][0], 64), 4)
    m["gnwT"] = colT(inputs["gnorm_w"][0], 4)
    m["w_out"] = f32(inputs["w_out"][0])
    m["w_gu"] = f32(inputs["w_gu"][0])
    m["w_down"] = f32(inputs["w_down"][0])
    m["fnw"] = f32(inputs["final_norm_w"][None])
    cm = np.ones((128, 2, 642), np.float32)
    ii = np.arange(642)
    cm[:, 0, ii % 64 == 0] = 0.0
    cm[:, 1, ii % 64 == 1] = 0.0
    m["cmask"] = cm
    return m


S_ = slice(None)
C0 = 0.6065306597126334
TBLOCKS = [(0, 256)] + [(256 + 512 * i, 512) for i in range(4)]
ORDER_F = list(range(NCH))
ORDER_B = [1, 0] + list(range(NCH - 1, 1, -1))


USE_F32R = False


def FR(v):
    if not USE_F32R:
        return v
    return View(v.ap.bitcast(mybir.dt.float32r), v.trks)


def sl(a, n):
    return slice(a, a + n)


def rw_setup(G, es):
    P, I, nc = G.P, G.I, G.nc

    def sbl(name, shape, dt=F32, n=1):
        return Buf(es.enter_context(SBT(nc, "rw_" + name, list(shape), dt)), n)

    R = Ctx()
    G.R = R
    R.tsh = sbl("tshT", [128, 15, 3])
    R.w0 = sbl("w0T", [128, 4, 2])
    R.a0 = sbl("a0T", [128, 4, 2])
    R.kk = sbl("kkT", [128, 4])
    R.ka = sbl("kaT", [128, 4])
    R.omka = sbl("omkaT", [128, 4])
    R.rk = sbl("rkT", [128, 4])
    R.lnw = sbl("lnwT", [128, 4])
    R.lnb = sbl("lnbT", [128, 4])
    R.w2 = sbl("w2b", [128, 512], BF16)
    R.a2 = sbl("a2b", [128, 512], BF16)
    R.g2 = sbl("g2b", [128, 512], BF16)
    R.bones = sbl("bonesb", [128, 128], BF16)
    R.masks_f = sbl("masksf", [128, 4, 128])
    R.masks_b = sbl("masksb", [128, 4, 128], BF16)
    R.m4 = sbl("m4", [128, 2, 4, 128], BF16)
    R.rmask = sbl("rmask", [128, 512])
    for dst, nm in ((R.tsh, "tshiftT"), (R.w0, "w0T"), (R.a0, "a0T"), (R.kk, "kkT"), (R.ka, "kaT"), (R.rk, "rkT"),
                    (R.lnw, "lnwT"), (R.lnb, "lnbT"), (R.masks_f, "masks"), (R.rmask, "rmask")):
        P.dma("sp", dst(), I[nm]())
    for dst, nm in ((R.w2, "w2s"), (R.a2, "a2s"), (R.g2, "g2"), (R.bones, "bones")):
        P.dma("pool", dst(), I[nm]())
    P.op("dve", "tensor_scalar", out=R.omka(), in0=R.ka(), scalar1=-1.0, scalar2=1.0, op0=ALU.mult, op1=ALU.add)
    P.op("dve", "tensor_copy", out=R.masks_b(), in_=R.masks_f())
    for d, (strict, incl) in enumerate(((1, 0), (3, 2))):
        for q, mi in enumerate((strict, incl, strict, incl)):
            P.op("dve", "tensor_copy", out=R.m4((S_, d, q)), in_=R.masks_f((S_, mi)))
    R.diag = sbl("diag", [128, 12, 3, 128], BF16)
    for tl in range(12):
        for tap in range(3):
            P.op("dve", "tensor_scalar", out=R.diag((S_, tl, tap)), in0=G.ident_f(), scalar1=R.tsh((S_, tl, sl(tap, 1))),
                 scalar2=None, op0=ALU.mult)
    nseq = G.nseq
    R.rwF = G.dscr("rwF", [nseq, 2, 4, 512, T], BF16)
    R.rwT = G.dscr("rwT", [nseq, 5, T, 512], BF16)
    G.yTs = G.dscr("yTs", [nseq, 8 * 128, SEQ], BF16)
    R.gc = sbl("gc", [128, 4, 2, NCH])
    R.yacc = sbl("yacc", [128, 16, 512])


def phaseB1a(G, b):
    P, I, nc, R = G.P, G.I, G.nc, G.R
    with ExitStack() as es:
        def sbl(name, shape, dt=F32, n=1):
            return Buf(es.enter_context(SBT(nc, name, list(shape), dt)), n)

        def psl(name, shape, dt=F32, n=1):
            return Buf(es.enter_context(PST(nc, name, list(shape), dt)), n)

        uin_b = [sbl("uinb%d" % i, [128, 514], BF16) for i in range(6)]
        uin_f2 = [[sbl("uinf%d_%d" % (k, i), [128, 514]) for i in range(3)] for k in range(2)]
        usl = [sbl("usl%d" % i, [128, 512]) for i in range(3)]
        th = sbl("th", [128, 512], BF16)
        adb = sbl("adb", [128, 512], BF16)
        sg = sbl("sg", [128, 512], BF16)
        us_r2 = [sbl("us_r%d" % i, [128, 512]) for i in range(2)]
        us_k2 = [sbl("us_k%d" % i, [128, 512]) for i in range(2)]
        us_v2 = [sbl("us_v%d" % i, [128, 512]) for i in range(2)]
        sigw2 = [[sbl("sigw%d_%d" % (i, d), [128, 512]) for d in range(2)] for i in range(2)]
        aa2 = [[sbl("aa%d_%d" % (i, d), [128, 512]) for d in range(2)] for i in range(2)]
        kkn2 = [sbl("kkn%d" % i, [128, 512]) for i in range(2)]
        kkk, rn = sbl("kkk", [128, 512]), sbl("rn", [128, 512])
        sq = sbl("sq", [128, 512], BF16)
        rkb = sbl("rkb", [128, 512], BF16)
        Gs, Gi, Ge = (sbl(n, [128, 512]) for n in ("Gs", "Gi", "Ge"))
        eGi, eGe, eGn = (sbl(n, [128, 512]) for n in ("eGi", "eGe", "eGn"))
        kd, beta, tmp, tmp2 = (sbl(n, [128, 512]) for n in ("kd", "beta", "tmpb", "tmpc"))
        outs = [sbl("o%d" % i, [128, 512], BF16) for i in range(4)]
        FM = sbl("FM", [128, 5, 4, 512], BF16, n=5)
        tstg = [sbl("tstg%d" % i, [128, 512], BF16) for i in range(2)]
        psh = [psl("pshB%d" % i, [128, 512]) for i in range(3)]
        pl = [psl("plB%d" % i, [128, 512]) for i in range(2)]
        pn = [psl("pnB%d" % i, [128, 512]) for i in range(1)] * 2
        ptr = [psl("ptrB%d" % i, [128, 4, 128], BF16) for i in range(2)]
        st_ = {"io": 0, "itr": 0}
        v3 = lambda a: a.rearrange("p (c t) -> p c t", t=128)

        def geom(bi):
            t0, N = TBLOCKS[bi]
            seg0, seg1 = (0, CTX) if t0 < CTX else (CTX, T)
            return t0, N, seg0, seg1

        def load(bi, dst, src_rows, dram):
            t0, N, seg0, seg1 = geom(bi)
            lo = max(t0 - 1, seg0)
            hi = min(t0 + N + 1, seg1)
            c_lo = lo - (t0 - 1)
            if c_lo > 0:
                P.op("pool", "memset", ap=dst((S_, sl(0, c_lo))), constant=0.0)
            c_hi = c_lo + (hi - lo)
            if c_hi < N + 2:
                P.op("pool", "memset", ap=dst((S_, sl(c_hi, N + 2 - c_hi))), constant=0.0)
            P.dma("sp", dst((S_, sl(c_lo, hi - lo))), dram((b, src_rows, slice(lo, hi))))

        def loadj(bi, j):
            for q3 in range(3):
                load(bi, uin_b[3 * (j % 2) + q3], sl((4 * q3 + j) * 128, 128), G.uT)

        def loadf(bi):
            for jj in range(3):
                load(bi, uin_f2[bi % 2][jj], sl(jj * 128, 128), G.uTf)

        def front(bi, j):
            t0, N, _, _ = geom(bi)
            n_ = sl(0, N)
            jc = sl(j * 128, 128)
            jp = j % 2
            us_r, us_k, us_v = us_r2[jp], us_k2[jp], us_v2[jp]
            sigw, aa, kkn = sigw2[jp], aa2[jp], kkn2[jp]
            for q3, dst in enumerate((us_r, us_k, us_v)):
                src_ = uin_b[3 * jp + q3]
                for tap in range(3):
                    P.op("pe", "matmul", out=psh[q3]((S_, n_)), lhsT=R.diag((S_, 4 * q3 + j, tap)), rhs=src_((S_, sl(tap, N))),
                         start=(tap == 0), stop=(tap == 2))
                if q3 == 1:
                    P.op("dve", "tensor_copy", out=dst((S_, n_)), in_=psh[q3]((S_, n_)))
                else:
                    P.op("act", "activation", out=dst((S_, n_)), in_=psh[q3]((S_, n_)), func=AF.Copy)
                yield
            if j + 2 < 4:
                loadj(bi, j + 2)
            P.op("act", "activation", out=kkk((S_, n_)), in_=us_k((S_, n_)), func=AF.Identity, scale=R.kk((S_, sl(j, 1))))
            P.op("act", "activation", out=sq((S_, n_)), in_=kkk((S_, n_)), func=AF.Square)
            P.op("pe", "matmul", out=pn[0]((S_, n_)), lhsT=R.bones(), rhs=sq((S_, n_)), start=True, stop=True)
            yield
            for d in range(2):
                pr = sl(d * 64, 64)
                P.op("pe", "matmul", out=pl[0]((S_, n_)), lhsT=R.w2((pr, jc)), rhs=th((pr, n_)), start=True, stop=True)
                P.op("pe", "matmul", out=pl[1]((S_, n_)), lhsT=R.a2((pr, jc)), rhs=adb((pr, n_)), start=True, stop=True)
                P.op("act", "activation", out=sigw[d]((S_, n_)), in_=pl[0]((S_, n_)), func=AF.Sigmoid,
                     bias=R.w0((S_, j, sl(d, 1))))
                P.op("act", "activation", out=aa[d]((S_, n_)), in_=pl[1]((S_, n_)), func=AF.Sigmoid,
                     bias=R.a0((S_, j, sl(d, 1))))
                yield
            P.op("dve", "tensor_scalar", out=rn((S_, n_)), in0=pn[0]((S_, n_)), scalar1=1e-19, scalar2=None, op0=ALU.max)
            yield
            P.op("act", "activation", out=rn((S_, n_)), in_=rn((S_, n_)), func=AF.Ln)
            P.op("act", "activation", out=rn((S_, n_)), in_=rn((S_, n_)), func=AF.Exp, scale=-0.5)
            yield
            yield
            P.op("dve", "tensor_tensor", out=kkn((S_, n_)), in0=kkk((S_, n_)), in1=rn((S_, n_)), op=ALU.mult)
            yield

        def tail(bi, j):
            t0, N, _, _ = geom(bi)
            n_ = sl(0, N)
            nchk = N // 128
            lat0 = t0 - CTX
            jc = sl(j * 128, 128)
            jp = j % 2
            us_r, us_k, us_v = us_r2[jp], us_k2[jp], us_v2[jp]
            sigw, aa, kkn = sigw2[jp], aa2[jp], kkn2[jp]
            if t0 >= CTX:
                lt = sl(lat0, N)
                P.op("dve", "scalar_tensor_tensor", out=rkb((S_, n_)), in0=us_r((S_, n_)), scalar=R.rk((S_, sl(j, 1))),
                     in1=us_k((S_, n_)), op0=ALU.mult, op1=ALU.mult)
                P.op("pe", "matmul", out=pn[1]((S_, n_)), lhsT=R.bones(), rhs=rkb((S_, n_)), start=True, stop=True)
                P.op("dve", "tensor_tensor", out=R.bonusT((S_, j, lt)), in0=pn[1]((S_, n_)), in1=us_v((S_, n_)), op=ALU.mult)
                P.op("pe", "matmul", out=pn[1]((S_, n_)), lhsT=R.g2((S_, jc)), rhs=sg((S_, n_)), start=True, stop=True)
                P.op("act", "activation", out=R.gT((S_, j, lt)), in_=pn[1]((S_, n_)), func=AF.Copy)
            P.op("act", "activation", out=FM((S_, 4, j, n_), k=4), in_=us_v((S_, n_)), func=AF.Copy)
            yield
            for d in range(2):
                sw = sigw[d]
                P.op("dve", "tensor_tensor_scan", out=Gs((S_, n_)), data0=R.rmask((S_, n_)), data1=sw((S_, n_)),
                     initial=0.0, op0=ALU.mult, op1=ALU.add)
                if d == 0:
                    Gi_ = Gs
                else:
                    tot = V(Gs((S_, n_)), lambda a: v3(a)[:, :, 127:128].to_broadcast([128, nchk, 128]))
                    P.op("dve", "tensor_tensor", out=V(Ge((S_, n_)), v3), in0=tot, in1=V(Gs((S_, n_)), v3), op=ALU.subtract)
                    P.op("dve", "tensor_tensor", out=Gi((S_, n_)), in0=Ge((S_, n_)), in1=sw((S_, n_)), op=ALU.add)
                    Gi_ = Gi
                if d == 0:
                    P.op("dve", "tensor_tensor", out=Ge((S_, n_)), in0=Gi_((S_, n_)), in1=sw((S_, n_)), op=ALU.subtract)
                yield
                P.op("act", "activation", out=tmp((S_, n_)), in_=aa[d]((S_, n_)), func=AF.Identity,
                     scale=R.ka((S_, sl(j, 1))), bias=R.omka((S_, sl(j, 1))))
                P.op("act", "activation", out=eGi((S_, n_)), in_=Gi_((S_, n_)), func=AF.Exp, scale=-C0)
                P.op("act", "activation", out=eGe((S_, n_)), in_=Ge((S_, n_)), func=AF.Exp, scale=-C0)
                P.op("act", "activation", out=eGn((S_, n_)), in_=Gi_((S_, n_)), func=AF.Exp, scale=C0)
                P.op("dve", "tensor_tensor", out=kd((S_, n_)), in0=tmp((S_, n_)), in1=us_k((S_, n_)), op=ALU.mult)
                P.op("dve", "tensor_tensor", out=beta((S_, n_)), in0=kkn((S_, n_)), in1=aa[d]((S_, n_)), op=ALU.mult)
                yield
                c0i = t0 // 128
                col = 127 if d == 0 else 0
                P.op("dve", "tensor_copy", out=R.gc((S_, j, d, sl(c0i, nchk))),
                     in_=V(eGi((S_, n_)), lambda a: v3(a)[:, :, col]))
                oA, oR = outs[st_["io"] % 4], outs[(st_["io"] + 1) % 4]
                st_["io"] += 2
                oB, oK = FM((S_, 2 * d, j, n_), k=2 * d), FM((S_, 2 * d + 1, j, n_), k=2 * d + 1)
                P.op("dve", "scalar_tensor_tensor", out=oA((S_, n_)), in0=kkn((S_, n_)), scalar=-1.0, in1=eGe((S_, n_)),
                     op0=ALU.mult, op1=ALU.mult)
                P.op("dve", "tensor_tensor", out=oR((S_, n_)), in0=us_r((S_, n_)), in1=eGi((S_, n_)), op=ALU.mult)
                yield
                P.op("dve", "tensor_tensor", out=oB, in0=beta((S_, n_)), in1=eGn((S_, n_)), op=ALU.mult)
                P.op("dve", "tensor_tensor", out=oK, in0=kd((S_, n_)), in1=eGn((S_, n_)), op=ALU.mult)
                P.dma("sp", R.rwF((b, d, 0, jc, sl(t0, N))), oA((S_, n_)))
                P.dma("sp", R.rwF((b, d, 1, jc, sl(t0, N))), oR((S_, n_)))
                P.dma("sp", R.rwF((b, d, 2, jc, sl(t0, N))), oB)
                P.dma("sp", R.rwF((b, d, 3, jc, sl(t0, N))), oK)
                yield

        def lora_prefix(bi):
            t0, N, _, _ = geom(bi)
            n_ = sl(0, N)
            uf = uin_f2[bi % 2]
            for jj in range(3):
                j_ = 12 + jj
                dst, src_ = usl[jj], uf[jj]
                P.op("act", "activation", out=dst((S_, n_)), in_=src_((S_, sl(1, N))), func=AF.Identity,
                     scale=R.tsh((S_, j_, sl(1, 1))))
                P.op("dve", "scalar_tensor_tensor", out=dst((S_, n_)), in0=src_((S_, sl(0, N))),
                     scalar=R.tsh((S_, j_, sl(0, 1))), in1=dst((S_, n_)), op0=ALU.mult, op1=ALU.add)
                P.op("dve", "scalar_tensor_tensor", out=dst((S_, n_)), in0=src_((S_, sl(2, N))),
                     scalar=R.tsh((S_, j_, sl(2, 1))), in1=dst((S_, n_)), op0=ALU.mult, op1=ALU.add)
            P.op("act", "activation", out=th((S_, n_)), in_=usl[0]((S_, n_)), func=AF.Tanh)
            P.op("act", "activation", out=sg((S_, n_)), in_=usl[2]((S_, n_)), func=AF.Sigmoid)
            P.op("dve", "tensor_copy", out=adb((S_, n_)), in_=usl[1]((S_, n_)))

        def transposes(bi):
            t0, N, _, _ = geom(bi)
            for q in range(5):
                for c in range(N // 128):
                    pt = ptr[st_["itr"] % 2]
                    st = tstg[st_["itr"] % 2]
                    st_["itr"] += 1
                    for j in range(4):
                        P.op("pe", "transpose", out=pt((S_, j)), in_=FM((S_, q, j, sl(c * 128, 128)), k=q), identity=G.ident_b())
                    if st_["itr"] % 2:
                        P.op("act", "activation", out=st(), in_=V(pt(), lambda a: a.rearrange("p a b -> p (a b)")), func=AF.Copy)
                    else:
                        P.op("dve", "tensor_copy", out=st(), in_=V(pt(), lambda a: a.rearrange("p a b -> p (a b)")))
                    P.dma("sp", R.rwT((b, q, sl(t0 + c * 128, 128))), st())

        def rr(ga, gb):
            a_done = b_done = False
            while not (a_done and b_done):
                if not a_done:
                    a_done = next(ga, "end") == "end"
                if not b_done:
                    b_done = next(gb, "end") == "end"

        nb = len(TBLOCKS)
        loadf(0)
        loadj(0, 0)
        loadj(0, 1)
        lora_prefix(0)
        for _ in front(0, 0):
            pass
        for bi in range(nb):
            for j in range(4):
                rr(tail(bi, j), front(bi, j + 1) if j + 1 < 4 else iter(()))
            if bi + 1 < nb:
                loadf(bi + 1)
                loadj(bi + 1, 0)
                loadj(bi + 1, 1)
            transposes(bi)
            if bi + 1 < nb:
                lora_prefix(bi + 1)
                for _ in front(bi + 1, 0):
                    pass
        P.barrier()


def phaseB1b(G, b):
    P, I, nc, R = G.P, G.I, G.nc, G.R
    with ExitStack() as es:
        def sbl(name, shape, dt=F32, n=1):
            return Buf(es.enter_context(SBT(nc, name, list(shape), dt)), n)

        def psl(name, shape, dt=F32, n=1):
            return Buf(es.enter_context(PST(nc, name, list(shape), dt)), n)

        NB = 3
        cf = [sbl("cf%d" % i, [128, 4, 4, 128], BF16) for i in range(NB)]
        ct = [sbl("ct%d" % i, [128, 3, 512], BF16) for i in range(NB)]
        M1 = [sbl("M1_%d" % i, [128, 8, 512], BF16, n=8) for i in range(2)]
        TB = [[sbl("TB%d_%d" % (i, h), [128, 384], F32) for h in range(8)] for i in range(2)]
        TT = [sbl("TT%d" % i, [128, 8, 128], BF16, n=8) for i in range(2)]
        H = sbl("Hst", [128, 4, 64])
        Hb = sbl("Hbf", [128, 4, 64], BF16)
        Wsb = sbl("Wsb", [128, 512], BF16)
        Usb = sbl("Usb", [128, 512], BF16)
        pA = [psl("pA%d" % i, [128, 512]) for i in range(2)]
        pB = [psl("pB%d" % i, [128, 512]) for i in range(3)]
        pWU, pY = psl("pWU", [128, 512]), psl("pY", [128, 512])
        pH = psl("pH", [128, 4, 64])
        cnt = {"a": 0, "b": 0}
        def t_phase(d, c, slot):
            cf_, ct_ = cf[slot % NB], ct[slot % NB]
            M1_, TT_ = M1[slot % 2], TT[slot % 2]
            tsl = sl(c * 128, 128)
            for q in range(4):
                P.dma("sp", cf_((S_, S_, q)), V(R.rwF((b, d, q, S_, tsl)), lambda a: a.rearrange("(j p) t -> p j t", p=128)))
            P.dma("sp", ct_((S_, 0)), R.rwT((b, 2 * d, tsl)))
            P.dma("sp", ct_((S_, 1)), R.rwT((b, 2 * d + 1, tsl)))
            P.dma("sp", ct_((S_, 2)), R.rwT((b, 4, tsl)))
            yield
            mN = 3 if d == 0 else 1
            mT = 1 if d == 0 else 3
            cur = [TB[0][h] for h in range(8)]
            nxt = [TB[1][h] for h in range(8)]
            for h in range(8):
                j, pb = h // 2, sl((h % 2) * 64, 64)
                A_, B_, K_ = cf_((pb, j, 0)), cf_((pb, j, 2)), cf_((pb, j, 3))
                AR = cf_((pb, j, sl(0, 2)))
                p1 = pA[cnt["a"] % 2]
                cnt["a"] += 1
                P.op("pe", "matmul", out=p1((S_, sl(0, 256))), lhsT=B_, rhs=AR, start=True, stop=True)
                P.op("pe", "matmul", out=p1((S_, sl(256, 256))), lhsT=K_, rhs=AR, start=True, stop=True)
                p2 = pB[cnt["b"] % 3]
                cnt["b"] += 1
                P.op("pe", "matmul", out=p2((S_, sl(0, 128))), lhsT=A_, rhs=B_, start=True, stop=True)
                P.op("dve", "tensor_tensor", out=M1_((S_, h), k=h), in0=p1(),
                     in1=V(R.m4((S_, d)), lambda a: a.rearrange("p a b -> p (a b)")), op=ALU.mult)
                P.op("dve", "tensor_tensor", out=FR(cur[h]((S_, sl(128, 128)))), in0=p1((S_, sl(0, 128))),
                     in1=R.masks_f((S_, mT)), op=ALU.mult)
                P.op("dve", "tensor_tensor", out=FR(cur[h]((S_, sl(256, 128)))), in0=p2((S_, sl(0, 128))),
                     in1=R.masks_f((S_, mN)), op=ALU.mult)
                if h % 4 == 3:
                    yield
            for h in range(8):
                p2 = pB[cnt["b"] % 3]
                cnt["b"] += 1
                c_, n_ = cur[h], nxt[h]
                P.op("pe", "matmul", out=p2((S_, sl(128, 128))), lhsT=FR(c_((S_, sl(256, 128)))), rhs=FR(c_((S_, sl(128, 128)))),
                     start=True, stop=True)
                P.op("pe", "matmul", out=p2((S_, sl(256, 128))), lhsT=FR(c_((S_, sl(128, 128)))), rhs=FR(c_((S_, sl(256, 128)))),
                     start=True, stop=True)
                P.op("dve", "tensor_tensor", out=FR(n_((S_, sl(0, 128)))), in0=c_((S_, sl(128, 128))), in1=G.ident_f(), op=ALU.add)
                P.op("act", "activation", out=FR(n_((S_, sl(128, 256)))), in_=p2((S_, sl(128, 256))), func=AF.Copy)
                if h % 4 == 3:
                    yield
            cur, nxt = nxt, cur
            for k in range(1, 7):
                last = k == 6
                for h in range(8):
                    p2 = pB[cnt["b"] % 3]
                    cnt["b"] += 1
                    c_, n_ = cur[h], nxt[h]
                    w_ = 128 if (last or k == 5) else 256
                    P.op("pe", "matmul", out=p2((S_, sl(0, w_))), lhsT=FR(c_((S_, sl(256, 128)))), rhs=FR(c_((S_, sl(0, w_)))),
                         start=True, stop=True)
                    if not last:
                        P.op("pe", "matmul", out=p2((S_, sl(256, 128))), lhsT=FR(c_((S_, sl(128, 128)))), rhs=FR(c_((S_, sl(256, 128)))),
                             start=True, stop=True)
                        P.op("dve", "tensor_tensor", out=FR(n_((S_, sl(0, 128)))), in0=p2((S_, sl(0, 128))),
                             in1=c_((S_, sl(0, 128))), op=ALU.add)
                        if k == 5:
                            P.op("act", "activation", out=FR(n_((S_, sl(256, 128)))), in_=p2((S_, sl(256, 128))), func=AF.Copy)
                        else:
                            P.op("act", "activation", out=FR(n_((S_, sl(128, 256)))), in_=p2((S_, sl(128, 256))), func=AF.Copy)
                    else:
                        P.op("dve", "tensor_tensor", out=TT_((S_, h), k=h), in0=p2((S_, sl(0, 128))),
                             in1=c_((S_, sl(0, 128))), op=ALU.add)
                    if h % 4 == 3:
                        yield
                cur, nxt = nxt, cur

        def s_phase(d, c, slot):
            cf_, ct_ = cf[slot % NB], ct[slot % NB]
            M1_, TT_ = M1[slot % 2], TT[slot % 2]
            latent = c >= 2
            for h in range(8):
                j, pb, hc = h // 2, sl((h % 2) * 64, 64), sl(h * 64, 64)
                P.op("pe", "matmul", out=pWU((S_, hc)), lhsT=cf_((pb, j, 0)), rhs=Hb((pb, j)), start=True, stop=False)
                P.op("pe", "matmul", out=pWU((S_, hc)), lhsT=M1_((S_, h, sl(256, 128)), k=h), rhs=ct_((S_, 2, hc)),
                     start=False, stop=True)
            P.op("act", "activation", out=Wsb(), in_=pWU(), func=AF.Copy)
            yield
            for h in range(8):
                hc = sl(h * 64, 64)
                P.op("pe", "matmul", out=pWU((S_, hc)), lhsT=TT_((S_, h), k=h), rhs=Wsb((S_, hc)), start=True, stop=True)
            P.op("dve", "tensor_copy", out=Usb(), in_=pWU())
            yield
            if latent:
                for h in range(8):
                    j, pb, hc = h // 2, sl((h % 2) * 64, 64), sl(h * 64, 64)
                    P.op("pe", "matmul", out=pY((S_, hc)), lhsT=cf_((pb, j, 1)), rhs=Hb((pb, j)), start=True, stop=False)
                    P.op("pe", "matmul", out=pY((S_, hc)), lhsT=M1_((S_, h, sl(128, 128)), k=h), rhs=Usb((S_, hc)),
                         start=False, stop=False)
                    P.op("pe", "matmul", out=pY((S_, hc)), lhsT=M1_((S_, h, sl(384, 128)), k=h), rhs=ct_((S_, 2, hc)),
                         start=False, stop=True)
                if d == 0:
                    P.op("act", "activation", out=R.yacc((S_, c - 2)), in_=pY(), func=AF.Copy)
                else:
                    P.op("dve", "tensor_tensor", out=R.yacc((S_, c - 2)), in0=pY(), in1=R.yacc((S_, c - 2)), op=ALU.add)
            for h in range(8):
                j, pb, hc = h // 2, sl((h % 2) * 64, 64), sl(h * 64, 64)
                P.op("pe", "matmul", out=pH((pb, j)), lhsT=ct_((S_, 0, hc)), rhs=Usb((S_, hc)), start=True, stop=False)
                P.op("pe", "matmul", out=pH((pb, j)), lhsT=ct_((S_, 1, hc)), rhs=ct_((S_, 2, hc)), start=False, stop=True)
            P.op("dve", "tensor_tensor", out=H(), in0=pH(), in1=H(), op=ALU.add)
            for j in range(4):
                P.op("act", "activation", out=H((S_, j)), in_=H((S_, j)), func=AF.Identity, scale=R.gc((S_, j, d, sl(c, 1))))
            P.op("act", "activation", out=Hb(), in_=H(), func=AF.Copy)
            yield

        work = [(d, c) for d in range(2) for c in (ORDER_F if d == 0 else ORDER_B)]
        for _ in t_phase(work[0][0], work[0][1], 0):
            pass
        for i, (d, c) in enumerate(work):
            if c == (ORDER_F if d == 0 else ORDER_B)[0]:
                P.op("dve", "memset", ap=H(), constant=0.0)
                P.op("dve", "memset", ap=Hb(), constant=0.0)
            gs = s_phase(d, c, i)
            gt = t_phase(work[i + 1][0], work[i + 1][1], i + 1) if i + 1 < len(work) else iter(())
            tn = 0
            s_done = False
            while True:
                t_done = next(gt, "end") == "end"
                tn += 1
                if not s_done and (tn % 5 == 2 or t_done):
                    s_done = next(gs, "end") == "end"
                if t_done:
                    while not s_done:
                        s_done = next(gs, "end") == "end"
                    break
        P.barrier()


def phaseB1c(G, b):
    P, I, nc, R = G.P, G.I, G.nc, G.R
    with ExitStack() as es:
        def sbl(name, shape, dt=F32, n=1):
            return Buf(es.enter_context(SBT(nc, name, list(shape), dt)), n)

        def psl(name, shape, dt=F32, n=1):
            return Buf(es.enter_context(PST(nc, name, list(shape), dt)), n)

        sq = sbl("c_sq", [128, 512])
        s1, s2, mu, var = (sbl(n, [128, 8]) for n in ("c_s1", "c_s2", "c_mu", "c_var"))
        yc = sbl("c_yc", [128, 512])
        yn = [sbl("c_yn%d" % i, [128, 512], BF16) for i in range(2)]
        t1 = sbl("c_t1", [128, 4, 128])
        yob = [sbl("c_yob%d" % i, [128, 4, 128], BF16) for i in range(2)]
        lneps = sbl("c_eps", [128, 1])
        P.op("dve", "memset", ap=lneps(), constant=RW_LN_EPS)
        pt = [psl("c_pt%d" % i, [128, 4, 128], BF16) for i in range(2)]
        v3 = lambda a: a.rearrange("p (h n) -> p h n", n=64)
        for c in range(16):
            y = R.yacc((S_, c))
            P.op("dve", "tensor_reduce", out=s1(), in_=V(y, v3), axis=AX.X, op=ALU.add)
            P.op("act", "activation", out=sq(), in_=y, func=AF.Square)
            P.op("dve", "tensor_reduce", out=s2(), in_=V(sq(), v3), axis=AX.X, op=ALU.add)
            P.op("dve", "tensor_scalar", out=mu(), in0=s1(), scalar1=1.0 / 64, scalar2=None, op0=ALU.mult)
            P.op("dve", "tensor_tensor", out=var(), in0=mu(), in1=mu(), op=ALU.mult)
            P.op("dve", "scalar_tensor_tensor", out=var(), in0=s2(), scalar=1.0 / 64, in1=var(), op0=ALU.mult, op1=ALU.subtract)
            P.op("act", "activation", out=var(), in_=var(), func=AF.Sqrt, bias=lneps())
            P.op("dve", "reciprocal", out=var(), in_=var())
            bc = lambda a: a.unsqueeze(2).to_broadcast([128, 8, 64])
            P.op("dve", "tensor_tensor", out=V(yc(), v3), in0=V(y, v3), in1=V(mu(), bc), op=ALU.subtract)
            yn_ = yn[c % 2]
            P.op("dve", "tensor_tensor", out=V(yn_(), v3), in0=V(yc(), v3), in1=V(var(), bc), op=ALU.mult)
            pt_ = pt[c % 2]
            for j in range(4):
                P.op("pe", "transpose", out=pt_((S_, j)), in_=yn_((S_, sl(j * 128, 128))), identity=G.ident_b())
            tk = sl(c * 128, 128)
            for j in range(4):
                P.op("act", "activation", out=t1((S_, j)), in_=pt_((S_, j)), func=AF.Identity, scale=R.lnw((S_, sl(j, 1))),
                     bias=R.lnb((S_, sl(j, 1))))
            P.op("dve", "tensor_tensor", out=t1(), in0=t1(), in1=R.bonusT((S_, S_, tk)), op=ALU.add)
            yo_ = yob[c % 2]
            P.op("dve", "tensor_tensor", out=yo_(), in0=t1(), in1=R.gT((S_, S_, tk)), op=ALU.mult)
            P.dma("sp", V(G.yTs((b, sl(0, 512), tk)), lambda a: a.rearrange("(j p) t -> p j t", p=128)), yo_())
        P.barrier()


def mb_setup(G, es):
    P, I, nc = G.P, G.I, G.nc

    def sbl(name, shape, dt=F32, n=1):
        return Buf(es.enter_context(SBT(nc, "mb_" + name, list(shape), dt)), n)

    M = Ctx()
    G.M = M
    M.conv = sbl("conv", [128, 8, 9])
    M.convb = sbl("convb", [128, 8])
    M.dtb = sbl("dtb", [128, 16])
    M.negA = sbl("negA", [128, 16])
    M.dskip = sbl("dskip", [128, 4])
    M.gnw = sbl("gnw", [128, 4])
    M.ones_f = sbl("ones_f", [128, 128])
    M.ones_b = sbl("ones_b", [128, 128], BF16)
    M.onec = sbl("onec", [128, 1])
    M.cmask = sbl("cmask", [128, 2, 642], BF16)
    P.dma("pool", M.cmask(), I["cmask"]())
    for dst, nm in ((M.conv, "convT"), (M.convb, "convbT"), (M.dtb, "dtb"), (M.negA, "alog"), (M.dskip, "dskipT"),
                    (M.gnw, "gnwT")):
        P.dma("sp", dst(), I[nm]())
    P.op("act", "activation", out=M.negA(), in_=M.negA(), func=AF.Exp)
    P.op("dve", "tensor_scalar", out=M.negA(), in0=M.negA(), scalar1=-1.0, scalar2=None, op0=ALU.mult)
    P.op("dve", "memset", ap=M.ones_f(), constant=1.0)
    P.op("dve", "memset", ap=M.ones_b(), constant=1.0)
    P.op("dve", "memset", ap=M.onec(), constant=1.0)
    nseq = G.nseq
    M.mF = G.dscr("mF", [nseq, 4 * 128, T], BF16)
    M.mT = G.dscr("mT", [nseq, T, 768], BF16)
    M.yacc = G.R.yacc


def phaseB2a(G, b):
    P, I, nc, M = G.P, G.I, G.nc, G.M
    with ExitStack() as es:
        def sbl(name, shape, dt=F32, n=1):
            return Buf(es.enter_context(SBT(nc, name, list(shape), dt)), n)

        def psl(name, shape, dt=F32, n=1):
            return Buf(es.enter_context(PST(nc, name, list(shape), dt)), n)

        HW = 65
        uin = [sbl("m_uin%d" % i, [128, 512 + 2 * HW], BF16) for i in range(3)]
        pconv = [psl("m_pconv%d" % i, [128, 512]) for i in range(2)]
        um = [[sbl("m_um%d_%d" % (q, i), [128, 512 + 2 * HW], BF16) for i in range(2)] for q in range(2)]
        zin = [sbl("m_zin%d" % i, [128, 512], BF16) for i in range(4)]
        FMm = sbl("m_FM", [128, 8, 512], BF16, n=8)
        ptr = [psl("m_ptr%d" % i, [128, 6, 128], BF16) for i in range(2)]
        tst = [sbl("m_tst%d" % i, [128, 768], BF16) for i in range(2)]
        iu = 0
        itr = 0
        work = [(bi, i) for bi in range(len(TBLOCKS)) for i in range(8)]

        def issue_load(wi):
            bi, i = work[wi]
            t0, N = TBLOCKS[bi]
            seg0, seg1 = (0, CTX) if t0 < CTX else (CTX, T)
            u_ = uin[wi % 3]
            lo, hi = max(t0 - HW, seg0), min(t0 + N + HW, seg1)
            c_lo = lo - (t0 - HW)
            if c_lo > 0:
                P.op("pool", "memset", ap=u_((S_, sl(0, c_lo))), constant=0.0)
            c_hi = c_lo + hi - lo
            if c_hi < N + 2 * HW:
                P.op("pool", "memset", ap=u_((S_, sl(c_hi, N + 2 * HW - c_hi))), constant=0.0)
            P.dma("sp", u_((S_, sl(c_lo, hi - lo))), G.uT((b, sl((19 + i) * 128, 128), slice(lo, hi))))

        issue_load(0)
        issue_load(1)
        wi = 0
        for bi_, (t0, N) in enumerate(TBLOCKS):
            seg0, seg1 = (0, CTX) if t0 < CTX else (CTX, T)
            n_ = sl(0, N)
            lat = t0 >= CTX
            if lat:
                for i in range(4):
                    P.dma("sp", zin[i]((S_, n_)), G.uT((b, sl((15 + i) * 128, 128), sl(t0, N))))
            for i in range(8):
                u_ = uin[wi % 3]
                if wi + 2 < len(work):
                    issue_load(wi + 2)
                wi += 1
                iu += 1
                pc_ = pconv[iu % 2]
                if not lat:
                    taps = [(1, 1), (1, 0), (1, 2)]
                else:
                    taps = [(1, 1), (0, 1), (2, 1)] + [(kh, kw) for kh in range(3) for kw in (0, 2)]
                if lat:
                    um0, um1 = um[0][iu % 2], um[1][iu % 2]
                    P.op("dve", "tensor_tensor", out=um0(), in0=u_(), in1=M.cmask((S_, 0)), op=ALU.mult)
                    P.op("dve", "tensor_tensor", out=um1(), in0=u_(), in1=M.cmask((S_, 1)), op=ALU.mult)
                for ti, (kh, kw) in enumerate(taps):
                    dr, dc = (kh - 1 if lat else 0), kw - 1
                    lhs = M.cdiag((S_, i, kh * 3 + kw))
                    start = HW + 64 * dr + dc
                    if not lat or dc == 0:
                        srcb = u_
                    else:
                        srcb = um0 if dc == -1 else um1
                    P.op("pe", "matmul", out=pc_((S_, n_)), lhsT=lhs, rhs=srcb((S_, sl(start, N))), start=(ti == 0),
                         stop=(ti == len(taps) - 1))
                P.op("act", "activation", out=FMm((S_, i, n_), k=i), in_=pc_((S_, n_)), func=AF.Silu,
                     bias=M.convb((S_, sl(i, 1))))
                if i < 4 and lat:
                    P.op("pool", "tensor_copy", out=M.xsT((S_, i, sl(t0 - CTX, N))), in_=FMm((S_, i, n_), k=i))
                if i >= 4:
                    P.dma("sp", M.mF((b, sl((i - 4) * 128, 128), sl(t0, N))), FMm((S_, i, n_), k=i))
            if lat:
                for i in range(4):
                    z_ = zin[i]
                    P.op("act", "activation", out=M.zsT((S_, i, sl(t0 - CTX, N))), in_=z_((S_, n_)), func=AF.Silu)
            for c in range(N // 128):
                pt, st = ptr[itr % 2], tst[itr % 2]
                itr += 1
                for i in range(6):
                    P.op("pe", "transpose", out=pt((S_, i)), in_=FMm((S_, i, sl(c * 128, 128)), k=i), identity=G.ident_b())
                P.op("dve", "tensor_copy", out=st(), in_=V(pt(), lambda a: a.rearrange("p a b -> p (a b)")))
                P.dma("sp", M.mT((b, sl(t0 + c * 128, 128))), st())
        P.barrier()


def phaseB2b(G, b):
    P, I, nc, M, R = G.P, G.I, G.nc, G.M, G.R
    with ExitStack() as es:
        def sbl(name, shape, dt=F32, n=1):
            return Buf(es.enter_context(SBT(nc, name, list(shape), dt)), n)

        def psl(name, shape, dt=F32, n=1):
            return Buf(es.enter_context(PST(nc, name, list(shape), dt)), n)

        NB = 3
        mf = [[sbl("s_mf%d_%d" % (dd, i), [128, 4, 128], BF16) for i in range(NB)] for dd in range(2)]
        mt = [[sbl("s_mt%d_%d" % (dd, i), [128, 768], BF16) for i in range(NB)] for dd in range(2)]
        dtr = [[sbl("s_dtr%d_%d" % (dd, i), [128, 16]) for i in range(NB)] for dd in range(2)]
        sm = [[[sbl("s_%s%d_%d" % (n, dd, i), [128, 8]) for n in ("xx", "ee", "dt", "la", "acs", "nacs", "dec", "etot")]
               for i in range(2)] for dd in range(2)]
        X = [[sbl("s_X%d_%d" % (dd, i), [128, 512], BF16) for i in range(2)] for dd in range(2)]
        Xd = [[sbl("s_Xd%d_%d" % (dd, i), [128, 512], BF16) for i in range(2)] for dd in range(2)]
        cbm = [sbl("s_cbm%d" % i, [128, 128], BF16) for i in range(2)]
        LM = [sbl("s_LM%d" % i, [128, 4, 128]) for i in range(2)]
        Lm = [sbl("s_L%d" % i, [128, 4, 128]) for i in range(2)]
        EA = [sbl("s_EA%d" % i, [128, 4, 128]) for i in range(2)]
        Wm = [[[sbl("s_W%d_%d_%d" % (dd, i, g), [128, 4, 128], BF16) for g in range(2)] for i in range(2)] for dd in range(2)]
        Cp = [[[sbl("s_Cp%d_%d_%d" % (dd, i, g), [128, 4, 128], BF16) for g in range(2)] for i in range(2)] for dd in range(2)]
        S2 = [sbl("s_S%d" % dd, [128, 8, 64]) for dd in range(2)]
        Sb2 = [sbl("s_Sb%d" % dd, [128, 8, 64], BF16) for dd in range(2)]
        yv = V(M.yacc(), lambda a: a.rearrange("p a b -> p (a b)").rearrange("p (j t) -> p j t", j=4))
        pa = psl("s_pa", [128, 16])
        pcb = [psl("s_pcb%d" % i, [128, 128]) for i in range(2)]
        pR = [psl("s_pR%d" % i, [128, 4, 128]) for i in range(2)]
        pY = psl("s_pY", [128, 4, 128])
        pS = psl("s_pS", [128, 512])
        h3 = lambda a: a.rearrange("p (h n) -> p h n", n=64)
        bc64 = lambda a: a.unsqueeze(2).to_broadcast([128, 8, 64])
        f2 = lambda a: a.rearrange("p a b -> p (a b)")
        for cc in range(16):
            P.op("dve", "memset", ap=M.yacc((S_, cc)), constant=0.0)
        kk = {"g": 0}

        def indep(d, c, slot):
            mi = 0 if d == 0 else 2
            dc = sl(d * 8, 8)
            mf_, mt_, dtr_ = mf[d][slot % NB], mt[d][slot % NB], dtr[d][slot % NB]
            xx, ee, dt_, la, acs, nacs, dec, etot = sm[d][slot % 2]
            X_, Xd_ = X[d][slot % 2], Xd[d][slot % 2]
            tsl = sl(c * 128, 128)
            latent = c >= 2
            P.dma("sp", mf_(), V(M.mF((b, S_, tsl)), lambda a: a.rearrange("(q p) t -> p q t", p=128)))
            P.dma("sp", mt_(), M.mT((b, tsl)))
            P.dma("sp", dtr_(), G.dtt((b, tsl)))
            yield
            P.op("dve", "tensor_tensor", out=xx(), in0=dtr_((S_, dc)), in1=M.dtb((S_, dc)), op=ALU.add)
            P.op("act", "activation", out=ee(), in_=xx(), func=AF.Exp)
            P.op("act", "activation", out=dt_(), in_=ee(), func=AF.Ln, bias=M.onec())
            P.op("dve", "tensor_tensor", out=la(), in0=dt_(), in1=M.negA((S_, dc)), op=ALU.mult)
            P.op("pe", "matmul", out=pa((S_, sl(0, 8))), lhsT=R.masks_f((S_, mi)), rhs=la(), start=True, stop=True)
            P.op("pe", "matmul", out=pa((S_, sl(8, 8))), lhsT=M.ones_f(), rhs=la(), start=True, stop=True)
            P.op("dve", "tensor_copy", out=acs(), in_=pa((S_, sl(0, 8))))
            P.op("dve", "tensor_tensor", out=dec(), in0=pa((S_, sl(8, 8))), in1=acs(), op=ALU.subtract)
            P.op("act", "activation", out=dec(), in_=dec(), func=AF.Exp)
            P.op("act", "activation", out=etot(), in_=pa((S_, sl(8, 8))), func=AF.Exp)
            P.op("dve", "tensor_tensor", out=V(X_(), h3), in0=V(mt_((S_, sl(0, 512))), h3), in1=V(dt_(), bc64), op=ALU.mult)
            P.op("dve", "tensor_tensor", out=V(Xd_(), h3), in0=V(X_(), h3), in1=V(dec(), bc64), op=ALU.mult)
            yield
            if latent:
                for g in range(2):
                    hs = sl(4 * g, 4)
                    P.op("pe", "matmul", out=pcb[d](), lhsT=mf_((S_, g)), rhs=mf_((S_, 2 + g)), start=True, stop=True)
                    P.op("dve", "tensor_tensor", out=LM[d](),
                         in0=V(R.masks_f((S_, mi)), lambda a: a.unsqueeze(1).to_broadcast([128, 4, 128])),
                         in1=V(la((S_, hs)), lambda a: a.unsqueeze(2).to_broadcast([128, 4, 128])), op=ALU.mult)
                    P.op("pe", "matmul", out=V(pR[d](), f2), lhsT=M.ones_f(), rhs=V(LM[d](), f2), start=True, stop=True)
                    P.op("dve", "tensor_tensor", out=cbm[d](), in0=pcb[d](), in1=R.masks_b((S_, mi)), op=ALU.mult)
                    yield
                    for hh in range(4):
                        P.op("dve", "tensor_scalar", out=Lm[d]((S_, hh)), in0=pR[d]((S_, hh)),
                             scalar1=acs((S_, sl(4 * g + hh, 1))), scalar2=0.0, op0=ALU.subtract, op1=ALU.min)
                    P.op("act", "activation", out=Lm[d](), in_=Lm[d](), func=AF.Exp)
                    P.op("act", "activation", out=EA[d](), in_=pR[d](), func=AF.Exp)
                    yield
                    P.op("dve", "tensor_tensor", out=Wm[d][slot % 2][g](), in0=Lm[d](),
                         in1=V(cbm[d](), lambda a: a.unsqueeze(1).to_broadcast([128, 4, 128])), op=ALU.mult)
                    P.op("dve", "tensor_tensor", out=Cp[d][slot % 2][g](), in0=EA[d](),
                         in1=V(mf_((S_, 2 + g)), lambda a: a.unsqueeze(1).to_broadcast([128, 4, 128])), op=ALU.mult)
                    yield

        def dep(d, c, slot):
            mt_ = mt[d][slot % NB]
            etot = sm[d][slot % 2][7]
            X_, Xd_ = X[d][slot % 2], Xd[d][slot % 2]
            S, Sb = S2[d], Sb2[d]
            latent = c >= 2
            if latent:
                for g in range(2):
                    for hh in range(4):
                        h = 4 * g + hh
                        j, pb, hc = h // 2, sl((h % 2) * 64, 64), sl(h * 64, 64)
                        P.op("pe", "matmul", out=pY((pb, j)), lhsT=X_((S_, hc)), rhs=Wm[d][slot % 2][g]((S_, hh)), start=True, stop=False)
                        P.op("pe", "matmul", out=pY((pb, j)), lhsT=Sb((S_, h)), rhs=Cp[d][slot % 2][g]((S_, hh)), start=False, stop=True)
                yo = V(yv, lambda a: a[:, :, (c - 2) * 128:(c - 1) * 128])
                P.op("dve", "tensor_tensor", out=yo, in0=pY(), in1=yo, op=ALU.add)
            yield
            for g in range(2):
                P.op("pe", "matmul", out=pS((S_, sl(g * 256, 256))), lhsT=mt_((S_, sl(512 + g * 128, 128))),
                     rhs=Xd_((S_, sl(g * 256, 256))), start=True, stop=True)
            P.op("dve", "tensor_tensor", out=S(), in0=S(), in1=V(etot(), bc64), op=ALU.mult)
            P.op("dve", "tensor_tensor", out=V(S(), lambda a: a.rearrange("p h n -> p (h n)")), in0=pS(),
                 in1=V(S(), lambda a: a.rearrange("p h n -> p (h n)")), op=ALU.add)
            P.op("act", "activation", out=Sb(), in_=S(), func=AF.Copy)
            yield

        def stream(d):
            order = ORDER_F if d == 0 else ORDER_B
            for _ in indep(d, order[0], 0):
                yield
            P.op("dve", "memset", ap=S2[d](), constant=0.0)
            P.op("dve", "memset", ap=Sb2[d](), constant=0.0)
            for i, c in enumerate(order):
                gd = dep(d, c, i)
                gi = indep(d, order[i + 1], i + 1) if i + 1 < len(order) else iter(())
                i_done = d_done = False
                while not (i_done and d_done):
                    if not i_done:
                        i_done = next(gi, "end") == "end"
                    if not d_done:
                        d_done = next(gd, "end") == "end"
                    yield

        g0, g1 = stream(0), stream(1)
        a_done = b_done = False
        while not (a_done and b_done):
            if not a_done:
                a_done = next(g0, "end") == "end"
            if not b_done:
                b_done = next(g1, "end") == "end"
        P.barrier()


def phaseB2c(G, b):
    P, I, nc, M = G.P, G.I, G.nc, G.M
    with ExitStack() as es:
        def sbl(name, shape, dt=F32, n=1):
            return Buf(es.enter_context(SBT(nc, name, list(shape), dt)), n)

        def psl(name, shape, dt=F32, n=1):
            return Buf(es.enter_context(PST(nc, name, list(shape), dt)), n)

        yv = V(M.yacc(), lambda a: a.rearrange("p a b -> p (a b)").rearrange("p (j t) -> p j t", j=4))
        y = sbl("f_y", [128, 4, 512])
        sq = sbl("f_sq", [128, 4, 512], BF16)
        rs = sbl("f_rs", [128, 2, 512])
        yob = [sbl("f_yob%d" % i, [128, 4, 512], BF16) for i in range(2)]
        pq = [psl("f_pq%d" % i, [128, 512]) for i in range(2)]
        for q in range(4):
            tk = sl(q * 512, 512)
            yq = V(yv, lambda a: a[:, :, q * 512:(q + 1) * 512])
            P.op("dve", "tensor_tensor", out=y(), in0=M.xsT((S_, S_, tk)),
                 in1=V(M.dskip(), lambda a: a.unsqueeze(2).to_broadcast([128, 4, 512])), op=ALU.mult)
            P.op("dve", "tensor_tensor", out=y(), in0=y(), in1=yq, op=ALU.add)
            P.op("dve", "tensor_tensor", out=y(), in0=y(), in1=M.zsT((S_, S_, tk)), op=ALU.mult)
            P.op("act", "activation", out=sq(), in_=y(), func=AF.Square)
            for g in range(2):
                P.op("pe", "matmul", out=pq[g](), lhsT=M.ones_b(), rhs=sq((S_, 2 * g)), start=True, stop=False)
                P.op("pe", "matmul", out=pq[g](), lhsT=M.ones_b(), rhs=sq((S_, 2 * g + 1)), start=False, stop=True)
                P.op("act", "activation", out=rs((S_, g)), in_=pq[g](), func=AF.Sqrt, bias=G.epsc(), scale=1.0 / 256)
            P.op("dve", "reciprocal", out=rs(), in_=rs())
            for g in range(2):
                P.op("dve", "tensor_tensor", out=y((S_, sl(2 * g, 2))), in0=y((S_, sl(2 * g, 2))),
                     in1=V(rs((S_, g)), lambda a: a.unsqueeze(1).to_broadcast([128, 2, 512])), op=ALU.mult)
            yo_ = yob[q % 2]
            for j in range(4):
                P.op("act", "activation", out=yo_((S_, j)), in_=y((S_, j)), func=AF.Identity, scale=M.gnw((S_, sl(j, 1))))
            P.dma("sp", V(G.yTs((b, sl(512, 512), tk)), lambda a: a.rearrange("(j p) t -> p j t", p=128)), yo_())
        P.barrier()


def phaseB3(G, b):
    P, I, nc, M, R = G.P, G.I, G.nc, G.M, G.R
    with ExitStack() as es:
        def sbl(name, shape, dt=F32, n=1):
            return Buf(es.enter_context(SBT(nc, name, list(shape), dt)), n)

        def psl(name, shape, dt=F32, n=1):
            return Buf(es.enter_context(PST(nc, name, list(shape), dt)), n)

        xt = [sbl("o_xt%d" % i, [128, D]) for i in range(3)]
        x1 = [sbl("o_x1%d" % i, [128, D]) for i in range(2)]
        yt = [sbl("o_yt%d" % i, [128, 8, 128], BF16) for i in range(3)]
        gt = sbl("o_gt", [128, D])
        wout = sbl("o_wout", [128, 8, D], BF16)
        wv = V(I["w_out"](), lambda a: a.rearrange("(k p) n -> p k n", p=128))
        for k in range(8):
            P.dma("pool", wout((S_, k)), V(wv, lambda a: a[:, k]))
        P.dma("sp", gt(), G.gts((b, 0)))
        po = [psl("o_po%d" % i, [128, 512]) for i in range(4)]
        def issue(i):
            P.dma("sp", xt[i % 3](), I["xall"]((b, sl(CTX + i * 128, 128))))
            P.dma("sp", yt[i % 3](), V(G.yTs((b, S_, sl(i * 128, 128))), lambda a: a.rearrange("(k p) t -> p k t", p=128)))

        issue(0)
        issue(1)
        for i in range(16):
            x_, x1_, yt_ = xt[i % 3], x1[i % 2], yt[i % 3]
            tk = sl(i * 128, 128)
            if i + 2 < 16:
                issue(i + 2)
            for hf in range(2):
                p_ = po[(2 * i + hf) % 4]
                for k in range(8):
                    P.op("pe", "matmul", out=p_(), lhsT=yt_((S_, k)), rhs=wout((S_, k, sl(hf * 512, 512))), start=(k == 0), stop=(k == 7))
                cs = sl(hf * 512, 512)
                P.op("dve", "tensor_tensor", out=x1_((S_, cs)), in0=p_(), in1=gt((S_, cs)), op=ALU.mult)
                P.op("dve", "tensor_tensor", out=x1_((S_, cs)), in0=x1_((S_, cs)), in1=x_((S_, cs)), op=ALU.add)
            P.dma("sp", G.x1s((b, tk)), x1_())
        P.barrier()


def phaseC(G):
    P, I, nc, nseq = G.P, G.I, G.nc, G.nseq
    with ExitStack() as es:
        def sbl(name, shape, dt=F32, n=1):
            return Buf(es.enter_context(SBT(nc, name, list(shape), dt)), n)

        def psl(name, shape, dt=F32, n=1):
            return Buf(es.enter_context(PST(nc, name, list(shape), dt)), n)

        NT = 512
        wgu = sbl("c_wgu", [128, 8, 2 * D_FF], BF16)
        wdn = sbl("c_wdn", [128, 22, D], BF16)
        fnw = sbl("c_fnw", [128, D])
        gv = V(I["w_gu"](), lambda a: a.rearrange("(k p) n -> p k n", p=128))
        for k in range(8):
            for c0 in range(0, 2 * D_FF, 1408):
                P.dma("pool", wgu((S_, k, sl(c0, 1408))), V(gv, lambda a: a[:, k, c0:c0 + 1408]))
        dv = V(I["w_down"](), lambda a: a.rearrange("(k p) n -> p k n", p=128))
        for k in range(22):
            P.dma("pool", wdn((S_, k)), V(dv, lambda a: a[:, k]))
        P.dma("sp", fnw(), V(I["fnw"](), lambda a: a.partition_broadcast(128)))
        xt = [sbl("c_xt%d" % i, [128, D]) for i in range(2)]
        xr = [sbl("c_xr%d" % i, [128, D]) for i in range(2)]
        gt = sbl("c_gt", [128, D])
        xn = [sbl("c_xn%d" % i, [128, D], BF16) for i in range(2)]
        junk = sbl("c_junk", [128, D], BF16)
        ss = [sbl("c_ss%d" % i, [128, 1]) for i in range(4)]
        rstd = [sbl("c_rstd%d" % i, [128, 1]) for i in range(4)]
        xm = [sbl("c_xm%d" % i, [128, 8, NT], BF16) for i in range(2)]
        act = sbl("c_act", [128, 22, NT], BF16)
        sg = [sbl("c_sg%d" % i, [128, NT], BF16) for i in range(2)]
        x2 = [sbl("c_x2%d" % i, [128, 512]) for i in range(1)] * 2
        tp = [psl("c_tp%d" % i, [128, 8, 128], BF16) for i in range(2)]
        pg = [psl("c_pg%d" % i, [128, NT]) for i in range(2)]
        pu = [psl("c_pu%d" % i, [128, NT]) for i in range(2)]
        pd = [psl("c_pd%d" % i, [128, 512]) for i in range(2)]
        st_ = {"it": 0, "ij": 0, "ipd": 0, "ie": 0}
        blocks = [(b, blk) for b in range(nseq) for blk in range(SEQ // NT)]

        def front(k):
            b, blk = blocks[k]
            xm_ = xm[k % 2]
            for il in range(NT // 128):
                tok0 = blk * NT + il * 128
                it = st_["it"]
                st_["it"] += 1
                x_ = xt[it % 2]
                xn_, ss_, rs_, tp_ = xn[it % 2], ss[it % 2], rstd[it % 2], tp[it % 2]
                P.dma("sp", x_(), G.x1s((b, sl(tok0, 128))))
                P.op("act", "activation", out=junk(), in_=x_(), func=AF.Square, accum_out=ss_())
                P.op("act", "activation", out=rs_(), in_=ss_(), func=AF.Sqrt, bias=G.epsc(), scale=1.0 / D)
                P.op("dve", "reciprocal", out=rs_(), in_=rs_())
                P.op("dve", "tensor_scalar", out=xn_(), in0=x_(), scalar1=rs_(), scalar2=None, op0=ALU.mult)
                yield
                for k8 in range(8):
                    P.op("pe", "transpose", out=tp_((S_, k8)), in_=xn_((S_, sl(k8 * 128, 128))), identity=G.ident_b())
                for k8 in range(8):
                    o = xm_((S_, k8, sl(il * 128, 128)))
                    s1 = G.scale2((S_, k8, sl(b, 1)))
                    s2 = G.modT((S_, 24 + k8, sl(b, 1)))
                    if k8 % 2 == 0:
                        P.op("dve", "tensor_scalar", out=o, in0=tp_((S_, k8)), scalar1=s1, scalar2=s2, op0=ALU.mult, op1=ALU.add)
                    else:
                        P.op("act", "activation", out=o, in_=tp_((S_, k8)), func=AF.Identity, bias=s2, scale=s1)
                yield
                yield

        def body(k):
            b, blk = blocks[k]
            xm_ = xm[k % 2]
            if blk == 0:
                P.dma("sp", gt(), G.gts((b, 1)))
            for j in range(22):
                ij = st_["ij"]
                st_["ij"] += 1
                pg_, pu_, sg_ = pg[ij % 2], pu[ij % 2], sg[ij % 2]
                for k8 in range(8):
                    P.op("pe", "matmul", out=pg_(), lhsT=wgu((S_, k8, sl(j * 128, 128))), rhs=xm_((S_, k8)), start=(k8 == 0), stop=(k8 == 7))
                for k8 in range(8):
                    P.op("pe", "matmul", out=pu_(), lhsT=wgu((S_, k8, sl(D_FF + j * 128, 128))), rhs=xm_((S_, k8)),
                         start=(k8 == 0), stop=(k8 == 7))
                P.op("act", "activation", out=sg_(), in_=pg_(), func=AF.Silu)
                P.op("dve", "tensor_tensor", out=act((S_, j)), in0=pu_(), in1=sg_(), op=ALU.mult)
                yield
            for il in range(NT // 128):
                tok0 = blk * NT + il * 128
                ie = st_["ie"]
                st_["ie"] += 1
                x_ = xr[ie % 2]
                ss_, rs_ = ss[2 + ie % 2], rstd[2 + ie % 2]
                P.dma("sp", x_(), G.x1s((b, sl(tok0, 128))))
                for hf in range(2):
                    p_ = pd[st_["ipd"] % 2]
                    st_["ipd"] += 1
                    cs = sl(hf * 512, 512)
                    for j in range(22):
                        P.op("pe", "matmul", out=p_(), lhsT=act((S_, j, sl(il * 128, 128))), rhs=wdn((S_, j, cs)),
                             start=(j == 0), stop=(j == 21))
                    x2_ = x2[hf]
                    P.op("dve", "tensor_tensor", out=x2_(), in0=p_(), in1=gt((S_, cs)), op=ALU.mult)
                    P.op("dve", "tensor_tensor", out=x_((S_, cs)), in0=x2_(), in1=x_((S_, cs)), op=ALU.add)
                    yield
                P.op("act", "activation", out=junk(), in_=x_(), func=AF.Square, accum_out=ss_())
                P.op("act", "activation", out=rs_(), in_=ss_(), func=AF.Sqrt, bias=G.epsc(), scale=1.0 / D)
                P.op("dve", "reciprocal", out=rs_(), in_=rs_())
                P.op("dve", "scalar_tensor_tensor", out=x_(), in0=x_(), scalar=rs_(), in1=fnw(), op0=ALU.mult, op1=ALU.mult)
                P.dma("sp", G.out((b, sl(tok0, 128))), x_())
                yield

        def rr(ga, gb, ratio=2):
            a_done = b_done = False
            while not (a_done and b_done):
                for _ in range(ratio):
                    if not a_done:
                        a_done = next(ga, "end") == "end"
                if not b_done:
                    b_done = next(gb, "end") == "end"

        for _ in front(0):
            pass
        for k in range(len(blocks)):
            rr(body(k), front(k + 1) if k + 1 < len(blocks) else iter(()), ratio=2)
        P.barrier()


def kernel(**inputs):
    inputs = {k: np.asarray(v) for k, v in inputs.items()}
    nc, G = build()
    in_maps = [host_inputs(inputs, i) for i in range(NCORES)]
    res = run_bass_kernel_spmd(nc, in_maps, core_ids=list(range(NCORES)))
    out = np.concatenate([np.asarray(r["out"]).reshape(NSEQ, SEQ, D) for r in res.results], axis=0)
    return np.ascontiguousarray(out, dtype=np.float32)


def seq_stack_b1(G):
    G.P.barrier()
    G.es_seq = ExitStack()
    nc = G.nc
    G.R.bonusT = Buf(G.es_seq.enter_context(SBT(nc, "rw_bonusT", [128, 4, SEQ], BF16)))
    G.R.gT = Buf(G.es_seq.enter_context(SBT(nc, "rw_gT", [128, 4, SEQ], BF16)))


def seq_stack_b2(G):
    G.P.barrier()
    G.es_seq = ExitStack()
    nc = G.nc
    G.M.xsT = Buf(G.es_seq.enter_context(SBT(nc, "mb_xsT", [128, 4, SEQ], BF16)))
    G.M.zsT = Buf(G.es_seq.enter_context(SBT(nc, "mb_zsT", [128, 4, SEQ], BF16)))
    if not hasattr(G.M, "cdiag_done"):
        pass
    G.M.cdiag = Buf(G.es_seq.enter_context(SBT(nc, "mb_cdiag", [128, 8, 9, 128], BF16)))
    for tl in range(8):
        for tap in range(9):
            G.P.op("dve", "tensor_scalar", out=G.M.cdiag((S_, tl, tap)), in0=G.ident_f(), scalar1=G.M.conv((S_, tl, sl(tap, 1))),
                   scalar2=None, op0=ALU.mult)
```
